# Optimizing a Trainium2 kernel written in Bass

```python
import jax, jax.numpy as jnp
from jax import lax
import numpy as np

D_MODEL = 4096
BATCH = 2
SEQ = 8192
DEPTH = 2

CONV_WIDTH = D_MODEL // 4
ATTN_WIDTH = D_MODEL // 2
POOL_WIDTH = D_MODEL // 4
HEAD_DIM = 64
N_Q_HEADS = ATTN_WIDTH // HEAD_DIM
N_KV_HEADS = N_Q_HEADS // 8
KV_WIDTH = N_KV_HEADS * HEAD_DIM
WINDOW = 128
BLOCK = 128
CONV_KERNEL = 31
POOL_WINDOWS = (2, 4, 8, 16)
N_POOL_GROUPS = 4
POOL_GROUP = POOL_WIDTH // N_POOL_GROUPS
NORM_EPS = 1e-5
LN_EPS = 1e-5

IN_SIZES = (CONV_WIDTH, CONV_WIDTH, CONV_WIDTH,
            ATTN_WIDTH, KV_WIDTH, KV_WIDTH, ATTN_WIDTH,
            POOL_WIDTH, POOL_WIDTH)
IN_WIDTH = sum(IN_SIZES)

kernel_name = "hybrid_conv_swa_pool_parallel_heads"


def _split_points():
    pts, acc = [], 0
    for s in IN_SIZES[:-1]:
        acc += s
        pts.append(acc)
    return pts


def rmsnorm(x, g):
    xf = x.astype(jnp.float32)
    y = xf * lax.rsqrt(jnp.mean(xf * xf, axis=-1, keepdims=True) + NORM_EPS)
    return (y * g.astype(jnp.float32)).astype(x.dtype)


def conformer_conv(a, b, dw, ln_g, ln_b, pw):
    u = a * jax.nn.sigmoid(b)
    y = lax.conv_general_dilated(
        u, dw[:, None, :].astype(u.dtype), window_strides=(1,),
        padding=[(CONV_KERNEL - 1, 0)],
        dimension_numbers=("NWC", "WIO", "NWC"),
        feature_group_count=CONV_WIDTH)
    yf = y.astype(jnp.float32)
    mu = jnp.mean(yf, axis=-1, keepdims=True)
    var = jnp.mean(jnp.square(yf - mu), axis=-1, keepdims=True)
    yn = (yf - mu) * lax.rsqrt(var + LN_EPS) * ln_g.astype(jnp.float32) + ln_b.astype(jnp.float32)
    return jax.nn.silu(yn).astype(a.dtype) @ pw


def window_attention(q, k, v, sinks):
    B, S, _ = q.shape
    nb = S // BLOCK
    G = N_Q_HEADS // N_KV_HEADS
    q = q.reshape(B, nb, BLOCK, N_KV_HEADS, G, HEAD_DIM)
    k = k.reshape(B, nb, BLOCK, N_KV_HEADS, HEAD_DIM)
    v = v.reshape(B, nb, BLOCK, N_KV_HEADS, HEAD_DIM)

    def with_prev(t):
        prev = jnp.concatenate([jnp.zeros_like(t[:, :1]), t[:, :-1]], axis=1)
        return jnp.concatenate([prev, t], axis=2)

    kk, vv = with_prev(k), with_prev(v)
    s = jnp.einsum("bnqkgd,bnskd->bnkgqs", q, kk,
                   preferred_element_type=jnp.float32) * (HEAD_DIM ** -0.5)
    qi = jnp.arange(BLOCK)[:, None]
    kj = jnp.arange(2 * BLOCK)[None, :]
    rel = qi + BLOCK - kj
    band = (rel >= 0) & (rel <= WINDOW)
    not_pad = (jnp.arange(nb)[:, None, None] > 0) | (kj[None] >= BLOCK)
    mask = band[None] & not_pad
    s = jnp.where(mask[None, :, None, None], s, -jnp.inf)
    sink = sinks.astype(jnp.float32).reshape(N_KV_HEADS, G)[None, None, :, :, None, None]
    m = jnp.maximum(jnp.max(s, axis=-1, keepdims=True), sink)
    p = jnp.exp(s - m)
    denom = jnp.sum(p, axis=-1, keepdims=True) + jnp.exp(sink - m)
    p = (p / denom).astype(v.dtype)
    o = jnp.einsum("bnkgqs,bnskd->bnqkgd", p, vv)
    return o.reshape(B, S, ATTN_WIDTH)


def multiscale_pool(u, w, scale):
    B, S, _ = u.shape
    uf = u.astype(jnp.float32).reshape(B, S, N_POOL_GROUPS, POOL_GROUP)
    cs = lax.cumsum(uf, axis=1)
    pos = jnp.arange(1, S + 1, dtype=jnp.float32)
    outs = []
    for g, win in enumerate(POOL_WINDOWS):
        c = cs[:, :, g]
        lagged = jnp.pad(c, ((0, 0), (win, 0), (0, 0)))[:, :S]
        mean = (c - lagged) / jnp.minimum(pos, float(win))[None, :, None]
        outs.append(mean - uf[:, :, g])
    mixed = jnp.stack(outs, axis=2).astype(u.dtype)
    y = jnp.einsum("bsgc,gcd->bsgd", mixed, w)
    return y.reshape(B, S, POOL_WIDTH) * scale


def hybrid_layer(x, norm_g, w_in, conv_dw, conv_ln_g, conv_ln_b, conv_pw,
                 attn_sinks, pool_w, pool_scale, w_out):
    h = rmsnorm(x, norm_g)
    proj = h @ w_in
    a, b, z_a, q, k, v, z_b, u_c, z_c = jnp.split(proj, _split_points(), axis=-1)
    y_a = conformer_conv(a, b, conv_dw, conv_ln_g, conv_ln_b, conv_pw) * jax.nn.silu(z_a)
    y_b = window_attention(q, k, v, attn_sinks) * jax.nn.silu(z_b)
    y_c = multiscale_pool(u_c, pool_w, pool_scale) * jax.nn.silu(z_c)
    y = jnp.concatenate([y_a, y_b, y_c], axis=-1)
    return x + y @ w_out


def setup_inputs(seed: int = 0) -> dict:
    key = jax.random.key(seed)
    ks = jax.random.split(key, 13)
    f32 = jnp.float32
    nrm = lambda k, shape: jax.random.normal(k, shape, dtype=f32)
    return {
        "x": nrm(ks[0], (BATCH, SEQ, D_MODEL)),
        "norm_g": 1.0 + 0.02 * nrm(ks[1], (DEPTH, D_MODEL)),
        "w_in": nrm(ks[2], (DEPTH, D_MODEL, IN_WIDTH)) * D_MODEL ** -0.5,
        "conv_dw": nrm(ks[3], (DEPTH, CONV_KERNEL, CONV_WIDTH)) * CONV_KERNEL ** -0.5,
        "conv_ln_g": 1.0 + 0.02 * nrm(ks[4], (DEPTH, CONV_WIDTH)),
        "conv_ln_b": 0.02 * nrm(ks[5], (DEPTH, CONV_WIDTH)),
        "conv_pw": nrm(ks[6], (DEPTH, CONV_WIDTH, CONV_WIDTH)) * CONV_WIDTH ** -0.5,
        "attn_sinks": nrm(ks[7], (DEPTH, N_Q_HEADS)),
        "pool_w": nrm(ks[8], (DEPTH, N_POOL_GROUPS, POOL_GROUP, POOL_GROUP)) * POOL_GROUP ** -0.5,
        "pool_scale": 1.0 + 0.02 * nrm(ks[9], (DEPTH, POOL_WIDTH)),
        "w_out": nrm(ks[10], (DEPTH, D_MODEL, D_MODEL)) * D_MODEL ** -0.5,
        "final_norm_g": 1.0 + 0.02 * nrm(ks[11], (D_MODEL,)),
    }


def reference(x, norm_g, w_in, conv_dw, conv_ln_g, conv_ln_b, conv_pw,
              attn_sinks, pool_w, pool_scale, w_out, final_norm_g):
    for l in range(DEPTH):
        x = hybrid_layer(x, norm_g[l], w_in[l], conv_dw[l], conv_ln_g[l], conv_ln_b[l],
                         conv_pw[l], attn_sinks[l], pool_w[l], pool_scale[l], w_out[l])
    return rmsnorm(x, final_norm_g)
```

```python
import numpy as np
from contextlib import ExitStack
import concourse.bass as bass
import concourse.mybir as mybir
from concourse.bass_utils import run_bass_kernel_spmd

F32 = mybir.dt.float32
BF16 = mybir.dt.bfloat16
AF = mybir.ActivationFunctionType
ALU = mybir.AluOpType

D = 4096
INW = 9728
KC = 32
NEG = -30000.0
COL_A, COL_B, COL_ZA, COL_Q, COL_K, COL_V, COL_ZB, COL_UC, COL_ZC = 0, 1024, 2048, 3072, 5120, 5376, 5632, 7680, 8704
NORM_EPS = 1e-5
LN_EPS = 1e-5
POOL_WINS = (2, 4, 8, 16)


class _Op:
    __slots__ = ("eng", "fn", "r", "w", "dma", "deps", "idx", "semval", "has_dep")

    def __init__(self, eng, fn, r=(), w=(), dma=None):
        self.eng, self.fn, self.r, self.w, self.dma = eng, fn, tuple(r), tuple(w), dma
        self.deps = None
        self.semval = 0
        self.has_dep = False


def _is_iv(t):
    return isinstance(t, tuple) and len(t) == 3 and t[0] == "@"


def _analyze(ops):
    last_w, readers, last_dma = {}, {}, {}
    ivals = []
    for i, op in enumerate(ops):
        op.idx = i
        deps = set()
        for x in op.r:
            if _is_iv(x):
                for a in ivals:
                    if a[3] and a[0] < x[2] and x[1] < a[1]:
                        deps.add(a[2])
            elif x in last_w:
                deps.add(last_w[x])
        for x in op.w:
            if _is_iv(x):
                for a in ivals:
                    if a[0] < x[2] and x[1] < a[1]:
                        deps.add(a[2])
            else:
                if x in last_w:
                    deps.add(last_w[x])
                deps.update(readers.get(x, ()))
        if op.dma is not None:
            if op.dma in last_dma:
                deps.add(last_dma[op.dma])
            last_dma[op.dma] = i
        deps.discard(i)
        op.deps = deps
        for x in op.w:
            if _is_iv(x):
                ivals = [a for a in ivals if not (x[1] <= a[0] and a[1] <= x[2])]
                ivals.append([x[1], x[2], i, True])
            else:
                last_w[x] = i
                readers[x] = []
        for x in op.r:
            if _is_iv(x):
                if op.dma is None:
                    ivals = [a for a in ivals if not ((not a[3]) and ops[a[2]].dma is None and ops[a[2]].eng == op.eng
                                                      and x[1] <= a[0] and a[1] <= x[2])]
                ivals.append([x[1], x[2], i, False])
                continue
            lst = readers.setdefault(x, [])
            if op.dma is None:
                for k in range(len(lst) - 1, -1, -1):
                    o = ops[lst[k]]
                    if o.dma is None and o.eng == op.eng:
                        del lst[k]
            lst.append(i)
    for op in ops:
        for d in op.deps:
            dop = ops[d]
            if dop.dma is None and dop.eng == "pe" and op.eng == "pe" and op.dma is None:
                continue
            dop.has_dep = True
    cnt, dcnt = {}, {}
    for op in ops:
        if op.dma is not None:
            dcnt[op.dma] = dcnt.get(op.dma, 0) + 16
            op.semval = dcnt[op.dma]
        elif op.has_dep:
            cnt[op.eng] = cnt.get(op.eng, 0) + 1
            op.semval = cnt[op.eng]
    return sorted(dcnt.keys(), key=str)


def _emit(nc, ops, es):
    dma_keys = _analyze(ops)
    esem = {e: es.enter_context(nc.semaphore("prog_" + e)) for e in ("pe", "act", "dve", "pool", "sp")}
    dsem = {k: es.enter_context(nc.semaphore("dma%d" % i)) for i, k in enumerate(dma_keys)}

    def run_engine(name, eng):
        waited = {}
        for op in ops:
            if op.eng != name:
                continue
            waits = {}
            for d in op.deps:
                dop = ops[d]
                if dop.dma is not None:
                    key = ("d", dop.dma)
                else:
                    if dop.eng == "pe" and name == "pe" and op.dma is None:
                        continue
                    key = ("e", dop.eng)
                if waits.get(key, 0) < dop.semval:
                    waits[key] = dop.semval
            for key, val in waits.items():
                if waited.get(key, 0) >= val:
                    continue
                waited[key] = val
                eng.wait_ge(dsem[key[1]] if key[0] == "d" else esem[key[1]], val)
            ins = op.fn(eng)
            if op.dma is not None:
                ins.then_inc(dsem[op.dma], 16)
            elif op.has_dep:
                ins.then_inc(esem[name], 1)
        fin = {}
        for op in ops:
            if op.eng == name and op.dma is not None:
                fin[op.dma] = op.semval
        for k, v in fin.items():
            if waited.get(("d", k), 0) < v:
                eng.wait_ge(dsem[k], v)

    with nc.Block() as blk:
        blk.tensor(lambda e: run_engine("pe", e))
        blk.scalar(lambda e: run_engine("act", e))
        blk.vector(lambda e: run_engine("dve", e))
        blk.gpsimd(lambda e: run_engine("pool", e))
        blk.sync(lambda e: run_engine("sp", e))


def build_program(n_real_blk=16, tb=6, n_layers=2):
    NBE = n_real_blk + 2
    TMAX = tb * 128
    nc = bass.Bass("TRN2", target_bir_lowering=False)
    x_ext = nc.dram_tensor("x_ext", [NBE * 128, D], F32, kind="ExternalInput").ap()
    w_in = nc.dram_tensor("w_in", [2, D, INW], F32, kind="ExternalInput").ap()
    conv_pw = nc.dram_tensor("conv_pw", [2, 1024, 1024], F32, kind="ExternalInput").ap()
    pool_w = nc.dram_tensor("pool_w", [2, 1024, 256], F32, kind="ExternalInput").ap()
    w_out = nc.dram_tensor("w_out", [2, D, D], F32, kind="ExternalInput").ap()
    norm_g = nc.dram_tensor("norm_g", [3, D], F32, kind="ExternalInput").ap()
    smallp = nc.dram_tensor("smallp", [128, 576], F32, kind="ExternalInput").ap()
    cpk = nc.dram_tensor("cpk", [128, 576], F32, kind="ExternalInput").ap()
    out = nc.dram_tensor("out", [n_real_blk * 128, D], F32, kind="ExternalOutput").ap()
    x1 = nc.dram_tensor("x1s", [NBE * 128, D], F32, kind="Internal").ap()
    x2 = nc.dram_tensor("x2s", [NBE * 128, D], F32, kind="Internal").ap()

    es = ExitStack()
    with es:
        def sb(name, shape, dt):
            return es.enter_context(nc.sbuf_tensor(name, shape, dt))

        hT = sb("hT", [128, KC, TMAX], BF16)
        yT = sb("yT", [128, KC, TMAX], BF16)
        NWS = 3
        Wt = sb("Wt", [128, NWS, KC, 128], BF16)
        cst = sb("cst", [128, 576], F32)
        smp = sb("smp", [128, 576], F32)
        identb = sb("identb", [128, 128], BF16)
        mask4 = sb("mask4", [128, 3, 512], BF16)
        onesEO = sb("onesEO", [128, 2, 64], BF16)
        onesf = sb("onesf", [128, 128], F32)
        sinkexp = sb("sinkexp", [128, 32], F32)
        kdup = sb("kdup", [128, 4, 128 + TMAX], BF16)
        Vt = sb("Vt", [128, tb + 1, 256], BF16)
        utail = sb("utail", [128, 8, 30], F32)
        uctail = sb("uctail", [128, 8, 16], F32)
        stat = sb("stat", [128, 4 * tb + 8], F32)
        ARENA_F32 = 15616
        arena = sb("arena", [128, ARENA_F32], F32)
        arena_b = arena[:, :].bitcast(BF16)

        def af(off, n):
            assert off + n <= ARENA_F32, (off, n)
            return arena[:, off:off + n]

        def ab(off, n):
            assert off * 2 + n <= 2 * ARENA_F32, (off, n)
            return arena_b[:, 2 * off:2 * off + n]

        PS = [es.enter_context(nc.psum_tensor("ps%d" % i, [128, 512], F32)) for i in range(8)]

        ops = []

        def op(eng, fn, r=(), w=(), dma=None):
            ops.append(_Op(eng, fn, r, w, dma))

        class AV:
            def __init__(self, off, n, dt=F32):
                self.ap = af(off, n) if dt == F32 else ab(off, n)
                self.tag = ("@", off, off + (n if dt == F32 else (n + 1) // 2))

        op("sp", lambda e: e.dma_start(out=cst[:, :], in_=cpk), w=["cst"], dma="cst")
        op("sp", lambda e: e.dma_start(out=smp[:, :], in_=smallp), w=["smp"], dma="smp")
        op("dve", lambda e: e.tensor_copy(identb[:, :], cst[:, 0:128]), r=["cst"], w=["identb"])

        def mk_mask(m):
            def f(e):
                src = cst[:, 128 * (m + 1):128 * (m + 2)]
                bc = bass.AP(src.tensor, src.offset, [src.ap[0], [0, 4], [1, 128]])
                return e.tensor_copy(mask4[:, m, :].rearrange("p (j q) -> p j q", q=128), bc)
            op("dve", f, r=["cst"], w=["mask4"])
        for m in range(3):
            mk_mask(m)
        op("dve", lambda e: e.memset(onesEO[:, :, :], 1.0), w=["onesEO"])
        op("dve", lambda e: e.memset(onesf[:, :], 1.0), w=["onesf"])
        op("dve", lambda e: e.memset(kdup[:, :, :], 0.0), w=["kdup_all"])
        op("dve", lambda e: e.memset(Vt[:, :, :], 0.0), w=["Vt_all"])
        op("dve", lambda e: e.memset(utail[:, :, :], 0.0), w=["utail_all"])
        op("dve", lambda e: e.memset(uctail[:, :, :], 0.0), w=["uctail_all"])
        op("dve", lambda e: e.tensor_scalar(out=smp[:, 0:496], in0=smp[:, 0:496], scalar1=0.5, scalar2=None, op0=ALU.mult),
           r=["smp"], w=["smp"])
        op("dve", lambda e: e.tensor_scalar(out=smp[:, 528:544], in0=smp[:, 528:544], scalar1=0.5, scalar2=None, op0=ALU.mult),
           r=["smp"], w=["smp"])
        op("act", lambda e: e.activation(out=sinkexp[:, :], in_=smp[:, 544:576], func=AF.Exp), r=["smp"], w=["sinkexp"])
        epsN = sb("epsN", [128, 1], F32)
        epsL = sb("epsL", [128, 1], F32)
        op("dve", lambda e: e.memset(epsN[:, :], NORM_EPS), w=["epsN"])
        op("dve", lambda e: e.memset(epsL[:, :], LN_EPS), w=["epsL"])

        SMP_DW = lambda l, c, j: smp[:, (l * 8 + c) * 31 + j:(l * 8 + c) * 31 + j + 1]
        SMP_LNG = lambda l, c: smp[:, 496 + l * 8 + c:496 + l * 8 + c + 1]
        SMP_LNB = lambda l, c: smp[:, 512 + l * 8 + c:512 + l * 8 + c + 1]
        SMP_PSC = lambda l, c: smp[:, 528 + l * 8 + c:528 + l * 8 + c + 1]

        state = {"ws": 0, "acc": 0, "aux": 0, "tmp": 0}
        TM = TMAX

        def next_w():
            s = state["ws"]
            state["ws"] = (s + 1) % NWS
            return s

        def next_acc_pair():
            a = state["acc"]
            state["acc"] = (a + 1) % 2
            return (2 * a, 2 * a + 1)

        def split2(lo, hi):
            h = (hi - lo) // 2
            return [(lo, h), (lo + h, hi - lo - h)]

        def q4(ap):
            return ap.rearrange("p (j q) -> p j q", q=128)

        TMPV = [AV(i * TM, TM) for i in range(4)]

        def next_tmp():
            i = state["tmp"]
            state["tmp"] = (i + 1) % 4
            return TMPV[i]
        BR0 = 4 * TM

        def norm_phase(xsrc, gidx, b0, nb):
            XH = [AV(i * 2048, 2048) for i in range(4)]
            HNH = [AV(8192, 2048, BF16), AV(9216, 2048, BF16)]
            GBV = AV(10240, 4096)
            JK = AV(14336, 2048, BF16)
            def load_gb():
                op("sp", lambda e: e.dma_start(out=GBV.ap, in_=norm_g[gidx:gidx + 1, :].partition_broadcast(128)),
                   w=[GBV.tag], dma="gb")

            def stage_stats(j):
                blk = b0 + j
                for h in range(2):
                    u = (2 * j + h) % 4
                    xt = XH[u]
                    op("sp", lambda e, xt=xt, blk=blk, h=h: e.dma_start(out=xt.ap, in_=xsrc[blk * 128:(blk + 1) * 128, h * 2048:(h + 1) * 2048]),
                       r=[("x", id(xsrc), blk, cb) for cb in range(8 * h, 8 * h + 8)], w=[xt.tag], dma=("xt", u))
                    ssh = stat[:, 3 * j + h:3 * j + h + 1]
                    op("act", lambda e, xt=xt, ssh=ssh: e.activation(out=JK.ap, in_=xt.ap, func=AF.Square, accum_out=ssh),
                       r=[xt.tag], w=[JK.tag, ("ssh", j, h)])
                ss = stat[:, 3 * j + 2:3 * j + 3]
                op("dve", lambda e, ss=ss, j=j: e.tensor_tensor(out=ss, in0=stat[:, 3 * j:3 * j + 1], in1=stat[:, 3 * j + 1:3 * j + 2], op=ALU.add),
                   r=[("ssh", j, 0), ("ssh", j, 1)], w=[("ss", j)])
                op("act", lambda e, ss=ss: e.activation(out=ss, in_=ss, func=AF.Sqrt, scale=1.0 / D, bias=epsN[:, 0:1]),
                   r=[("ss", j), "epsN"], w=[("ss", j)])
                op("dve", lambda e, ss=ss: e.reciprocal(out=ss, in_=ss), r=[("ss", j)], w=[("ss", j)])

            def stage_norm_tr(j):
                ss = stat[:, 3 * j + 2:3 * j + 3]
                for h in range(2):
                    xt = XH[(2 * j + h) % 4]
                    HN = HNH[h]
                    op("dve", lambda e, xt=xt, ss=ss, HN=HN, h=h: e.scalar_tensor_tensor(
                        out=HN.ap, in0=xt.ap, scalar=ss, in1=GBV.ap[:, h * 2048:(h + 1) * 2048], op0=ALU.mult, op1=ALU.mult),
                       r=[xt.tag, ("ss", j), GBV.tag], w=[HN.tag])
                    for qq in range(2):
                        q = 2 * h + qq
                        bank = 4 + (state["aux"] % 4)
                        state["aux"] += 1
                        pst = PS[bank][:, :].bitcast(BF16)

                        def ftr(e, qq=qq, pst=pst, HN=HN):
                            last = None
                            for i in range(8):
                                c = qq * 8 + i
                                last = e.transpose(pst[:, i * 128:(i + 1) * 128], HN.ap[:, c * 128:(c + 1) * 128], identb[:, :])
                            return last
                        op("pe", ftr, r=[HN.tag, "identb"], w=[("ps", bank)])
                        if qq == 0:
                            op("act", lambda e, q=q, pst=pst, j=j: e.activation(
                                out=hT[:, q * 8:(q + 1) * 8, j * 128:(j + 1) * 128],
                                in_=pst.rearrange("p (c t) -> p c t", t=128), func=AF.Copy),
                               r=[("ps", bank)], w=[("hT", j, q)])
                        else:
                            op("dve", lambda e, q=q, pst=pst, j=j: e.tensor_copy(
                                hT[:, q * 8:(q + 1) * 8, j * 128:(j + 1) * 128], pst.rearrange("p (c t) -> p c t", t=128)),
                               r=[("ps", bank)], w=[("hT", j, q)])

            stage_stats(0)
            load_gb()
            for j in range(nb):
                if j + 1 < nb:
                    stage_stats(j + 1)
                stage_norm_tr(j)

        def inproj_chunk(l, col0, nb, subs):
            s = next_w()
            wv = Wt[:, s, :, :]
            op("pool", lambda e: e.dma_start(out=wv, in_=w_in[l, :, col0:col0 + 128].rearrange("(k p) n -> p k n", p=128)),
               w=[("w", s)], dma=("w", s))
            banks = next_acc_pair()

            def f(e):
                last = None
                for kc in range(KC):
                    for si, (t0, n) in enumerate(subs):
                        last = e.matmul(PS[banks[si]][:, 0:n], lhsT=wv[:, kc, :], rhs=hT[:, kc, t0:t0 + n],
                                        start=(kc == 0), stop=(kc == KC - 1))
                return last
            op("pe", f, r=[("w", s)] + [("hT", j, q) for j in range(nb) for q in range(4)],
               w=[("ps", banks[0]), ("ps", banks[1])])
            return banks

        def gate_from_psum(banks, subs, dst, dst_tag):
            dst_tags = dst_tag if isinstance(dst_tag, list) else [dst_tag]
            tmp = next_tmp()
            for si, (t0, n) in enumerate(subs):
                op("act", lambda e, si=si, t0=t0, n=n: e.activation(out=tmp.ap[:, t0:t0 + n], in_=PS[banks[si]][:, 0:n], func=AF.Tanh, scale=0.5),
                   r=[("ps", banks[si])], w=[tmp.tag])
            for si, (t0, n) in enumerate(subs):
                op("dve", lambda e, si=si, t0=t0, n=n: e.scalar_tensor_tensor(
                    out=dst[:, t0:t0 + n], in0=tmp.ap[:, t0:t0 + n], scalar=1.0, in1=PS[banks[si]][:, 0:n], op0=ALU.add, op1=ALU.mult),
                   r=[tmp.tag, ("ps", banks[si])], w=dst_tags)

        oA = BR0
        A_U = [AV(oA + i * (TM + 30), TM + 30) for i in range(3)]; oA += 3 * (TM + 30)
        A_YC = [AV(oA + c * TM, TM) for c in range(8)]
        A_S = [AV(oA + c * TM, TM, BF16) for c in range(8)]; oA += 8 * TM
        A_MU = AV(oA, TM); oA += TM
        A_RS = AV(oA, TM); oA += TM
        A_PW = [AV(oA, 1024, BF16), AV(oA + 512, 1024, BF16)]; oA += 1024
        assert oA <= ARENA_F32, oA
        oB = BR0
        B_Q = [AV(oB, 4 * TM, BF16), AV(oB + 2 * TM, 4 * TM, BF16)]; oB += 4 * TM
        B_G = [AV(oB, 4 * TM), AV(oB + 4 * TM, 4 * TM)]; oB += 8 * TM
        B_PT = [AV(oB + 256 * i, 512, BF16) for i in range(4)]; oB += 1024
        B_DN = AV(oB, 512); oB += 512
        B_RG = AV(oB, 512); oB += 512
        assert oB <= ARENA_F32, oB
        UCW = TM + 16
        assert TM == 768 or True
        _uc_off = [BR0] + [BR0 + 4 * TM + k * UCW for k in range(3)]
        oC = BR0 + 4 * TM + 3 * UCW
        _uc_off += [oC + k * UCW for k in range(4)]; oC += 4 * UCW
        C_UC = [AV(o_, UCW) for o_ in _uc_off]
        C_PW = AV(BR0 + UCW, 2048, BF16)
        _s0 = BR0 + UCW + 1024
        assert _s0 + UCW <= BR0 + 4 * TM
        C_MX = [AV(oC + c * (TM // 2), TM, BF16) for c in range(8)]; oC += 4 * TM
        C_S = [AV(_s0, UCW), AV(oC, UCW)]; oC += UCW
        assert oC <= ARENA_F32, oC

        G_YT = [yT[:, 2 * m:2 * m + 2, :].rearrange("p c t -> p (c t)").bitcast(F32) for m in range(4)]
        G_YT_TAGS = [[("yT", 2 * m + k, j) for k in range(2) for j in range(tb)] for m in range(4)]

        def branch_a1(l, nb, t_lo):
            T = nb * 128
            subs = split2(0, T)
            pending_stats = []
            for c in range(8):
                U = A_U[c % 3]
                op("dve", lambda e, U=U, c=c: e.tensor_copy(U.ap[:, 0:30], utail[:, c, :]),
                   r=[("utail", c), "utail_all"], w=[U.tag])
                banks_b = inproj_chunk(l, COL_B + c * 128, nb, subs)
                tmp = next_tmp()
                for si, (t0, n) in enumerate(subs):
                    op("act", lambda e, si=si, t0=t0, n=n, tmp=tmp, banks_b=banks_b: e.activation(
                        out=tmp.ap[:, t0:t0 + n], in_=PS[banks_b[si]][:, 0:n], func=AF.Tanh, scale=0.5),
                       r=[("ps", banks_b[si])], w=[tmp.tag])
                banks_a = inproj_chunk(l, COL_A + c * 128, nb, subs)
                for si, (t0, n) in enumerate(subs):
                    op("dve", lambda e, si=si, t0=t0, n=n, tmp=tmp, U=U, banks_a=banks_a: e.scalar_tensor_tensor(
                        out=U.ap[:, 30 + t0:30 + t0 + n], in0=tmp.ap[:, t0:t0 + n], scalar=1.0, in1=PS[banks_a[si]][:, 0:n],
                        op0=ALU.add, op1=ALU.mult),
                       r=[tmp.tag, ("ps", banks_a[si])], w=[U.tag])
                while pending_stats:
                    pending_stats.pop(0)()
                Y = A_YC[c]
                op("dve", lambda e, U=U, Y=Y, c=c: e.tensor_scalar(out=Y.ap[:, t_lo:T], in0=U.ap[:, t_lo:T], scalar1=SMP_DW(l, c, 0), scalar2=None, op0=ALU.mult),
                   r=[U.tag, "smp"], w=[Y.tag])
                for j in range(1, 31):
                    if j == 8 and c % 2 == 1:
                        m = c // 2
                        subs_g = split2(t_lo, T)
                        banks_z = inproj_chunk(l, COL_ZA + m * 128, nb, subs_g)
                        gate_from_psum(banks_z, subs_g, G_YT[m], G_YT_TAGS[m])
                    op("dve", lambda e, U=U, Y=Y, c=c, j=j: e.scalar_tensor_tensor(
                        out=Y.ap[:, t_lo:T], in0=U.ap[:, t_lo + j:j + T], scalar=SMP_DW(l, c, j), in1=Y.ap[:, t_lo:T], op0=ALU.mult, op1=ALU.add),
                       r=[U.tag, "smp", Y.tag], w=[Y.tag])
                op("dve", lambda e, U=U, c=c: e.tensor_copy(utail[:, c, :], U.ap[:, T:T + 30]), r=[U.tag], w=[("utail", c)])
                def stats(Y=Y, c=c):
                    subs_s = split2(t_lo, T)
                    tmp = next_tmp()
                    op("act", lambda e: e.activation(out=tmp.ap[:, t_lo:T], in_=Y.ap[:, t_lo:T], func=AF.Square),
                       r=[Y.tag], w=[tmp.tag])

                    def fst(e):
                        last = None
                        for si, (t0, n) in enumerate(subs_s):
                            e.matmul(PS[4 + si][:, 0:n], lhsT=onesf[:, :], rhs=Y.ap[:, t0:t0 + n], start=(c == 0), stop=(c == 7))
                            last = e.matmul(PS[6 + si][:, 0:n], lhsT=onesf[:, :], rhs=tmp.ap[:, t0:t0 + n], start=(c == 0), stop=(c == 7))
                        return last
                    op("pe", fst, r=[Y.tag, tmp.tag, "onesf"], w=[("ps", 4), ("ps", 5), ("ps", 6), ("ps", 7)])
                pending_stats.append(stats)
            return pending_stats

        def branch_a2(l, nb, t_lo):
            T = nb * 128
            subs = split2(t_lo, T)
            gate_bufs = {}
            for m in range(4):
                gate_bufs[m] = (G_YT[m], G_YT_TAGS[m])
            for co in range(4, 7):
                banks_z = inproj_chunk(l, COL_ZA + co * 128, nb, subs)
                Ug = A_U[co - 4]
                gate_from_psum(banks_z, subs, Ug.ap, Ug.tag)
                gate_bufs[co] = (Ug.ap, [Ug.tag])
            mu, rs = A_MU, A_RS
            for si, (t0, n) in enumerate(subs):
                op("dve", lambda e, si=si, t0=t0, n=n: e.tensor_scalar(out=mu.ap[:, t0:t0 + n], in0=PS[4 + si][:, 0:n], scalar1=1.0 / 1024, scalar2=None, op0=ALU.mult),
                   r=[("ps", 4 + si)], w=[mu.tag])
            tmp = next_tmp()
            op("dve", lambda e, tmp=tmp: e.tensor_tensor(out=tmp.ap[:, t_lo:T], in0=mu.ap[:, t_lo:T], in1=mu.ap[:, t_lo:T], op=ALU.mult), r=[mu.tag], w=[tmp.tag])
            for si, (t0, n) in enumerate(subs):
                op("dve", lambda e, si=si, t0=t0, n=n, tmp=tmp: e.scalar_tensor_tensor(
                    out=rs.ap[:, t0:t0 + n], in0=PS[6 + si][:, 0:n], scalar=1.0 / 1024, in1=tmp.ap[:, t0:t0 + n], op0=ALU.mult, op1=ALU.subtract),
                   r=[("ps", 6 + si), tmp.tag], w=[rs.tag])
            op("act", lambda e: e.activation(out=rs.ap[:, t_lo:T], in_=rs.ap[:, t_lo:T], func=AF.Sqrt, bias=epsL[:, 0:1]), r=[rs.tag, "epsL"], w=[rs.tag])
            op("dve", lambda e: e.reciprocal(out=rs.ap[:, t_lo:T], in_=rs.ap[:, t_lo:T]), r=[rs.tag], w=[rs.tag])
            pending = None
            for c in range(8):
                Y = A_YC[c]
                S = A_S[c]
                op("dve", lambda e, Y=Y: e.tensor_tensor(out=Y.ap[:, t_lo:T], in0=Y.ap[:, t_lo:T], in1=mu.ap[:, t_lo:T], op=ALU.subtract),
                   r=[Y.tag, mu.tag], w=[Y.tag])
                op("dve", lambda e, Y=Y: e.tensor_tensor(out=Y.ap[:, t_lo:T], in0=Y.ap[:, t_lo:T], in1=rs.ap[:, t_lo:T], op=ALU.mult),
                   r=[Y.tag, rs.tag], w=[Y.tag])
                op("dve", lambda e, Y=Y, c=c: e.tensor_scalar(out=Y.ap[:, t_lo:T], in0=Y.ap[:, t_lo:T], scalar1=SMP_LNG(l, c), scalar2=SMP_LNB(l, c), op0=ALU.mult, op1=ALU.add),
                   r=[Y.tag, "smp"], w=[Y.tag])
                tmp = next_tmp()
                op("act", lambda e, Y=Y, tmp=tmp: e.activation(out=tmp.ap[:, t_lo:T], in_=Y.ap[:, t_lo:T], func=AF.Tanh, scale=0.5),
                   r=[Y.tag], w=[tmp.tag])
                if pending is not None:
                    pending()

                def comb(Y=Y, tmp=tmp, S=S):
                    op("dve", lambda e: e.scalar_tensor_tensor(
                        out=S.ap[:, t_lo:T], in0=tmp.ap[:, t_lo:T], scalar=1.0, in1=Y.ap[:, t_lo:T], op0=ALU.add, op1=ALU.mult),
                       r=[tmp.tag, Y.tag], w=[Y.tag])
                pending = comb
            pending()
            banks_z = inproj_chunk(l, COL_ZA + 7 * 128, nb, subs)
            gt7 = next_tmp()
            gate_from_psum(banks_z, subs, gt7.ap, gt7.tag)
            gate_bufs[7] = (gt7.ap, [gt7.tag])
            for co in range(8):
                gap, gtags = gate_bufs[co]
                PW = A_PW[co % 2]
                pwv = PW.ap.rearrange("p (k n) -> p k n", n=128)
                op("pool", lambda e, pwv=pwv, co=co: e.dma_start(out=pwv, in_=conv_pw[l, :, co * 128:(co + 1) * 128].rearrange("(k p) n -> p k n", p=128)),
                   w=[PW.tag], dma=("pws", co % 2))
                bk = (4, 5) if co % 2 == 0 else (6, 7)

                def fpw(e, pwv=pwv, bk=bk):
                    last = None
                    for ci in range(8):
                        for si, (t0, n) in enumerate(subs):
                            last = e.matmul(PS[bk[si]][:, 0:n], lhsT=pwv[:, ci, :], rhs=A_S[ci].ap[:, t0:t0 + n], start=(ci == 0), stop=(ci == 7))
                    return last
                op("pe", fpw, r=[PW.tag] + [A_YC[ci].tag for ci in range(8)], w=[("ps", bk[0]), ("ps", bk[1])])
                for si, (t0, n) in enumerate(subs):
                    op("dve", lambda e, si=si, t0=t0, n=n, bk=bk, gap=gap, co=co: e.scalar_tensor_tensor(
                        out=yT[:, co, t0:t0 + n], in0=PS[bk[si]][:, 0:n], scalar=0.25, in1=gap[:, t0:t0 + n], op0=ALU.mult, op1=ALU.mult),
                       r=[("ps", bk[si])] + gtags, w=[("yT", co, j) for j in range(tb)])

        def branch_b_kv(l, nb, prevT):
            T = nb * 128
            subs = split2(0, T)
            KD = [("kdup", h) for h in range(4)] + [("kdupx", h) for h in range(4)]
            op("dve", lambda e: e.tensor_copy(kdup[:, :, 0:128], kdup[:, :, prevT:prevT + 128]),
               r=["kdup_all"] + KD, w=["kdup_halo"])
            op("dve", lambda e: e.tensor_copy(Vt[:, 0, :], Vt[:, prevT // 128, :]),
               r=["Vt_all"] + [("Vt", j) for j in range(1, tb + 1)], w=[("Vt", 0)])
            for c in range(2):
                banks_k = inproj_chunk(l, COL_K + c * 128, nb, subs)
                for si, (t0, n) in enumerate(subs):
                    op("act", lambda e, si=si, t0=t0, n=n, c=c, banks_k=banks_k: e.activation(
                        out=kdup[0:64, 2 * c, 128 + t0:128 + t0 + n], in_=PS[banks_k[si]][0:64, 0:n], func=AF.Copy),
                       r=[("ps", banks_k[si]), "kdup_halo", "kdup_all"], w=[("kdup", 2 * c)])
                    op("act", lambda e, si=si, t0=t0, n=n, c=c, banks_k=banks_k: e.activation(
                        out=kdup[64:128, 2 * c + 1, 128 + t0:128 + t0 + n], in_=PS[banks_k[si]][64:128, 0:n], func=AF.Copy),
                       r=[("ps", banks_k[si]), "kdup_halo", "kdup_all"], w=[("kdup", 2 * c + 1)])
                op("sp", lambda e, c=c: e.dma_start(out=kdup[64:128, 2 * c, 128:128 + T], in_=kdup[0:64, 2 * c, 128:128 + T]),
                   r=[("kdup", 2 * c), "kdup_halo", "kdup_all"], w=[("kdupx", 2 * c)], dma=("kdupx", 2 * c))
                op("sp", lambda e, c=c: e.dma_start(out=kdup[0:64, 2 * c + 1, 128:128 + T], in_=kdup[64:128, 2 * c + 1, 128:128 + T]),
                   r=[("kdup", 2 * c + 1), "kdup_halo", "kdup_all"], w=[("kdupx", 2 * c + 1)], dma=("kdupx", 2 * c + 1))
            for c in range(2):
                s = next_w()
                wv = Wt[:, s, :, :]
                op("pool", lambda e, wv=wv, c=c: e.dma_start(out=wv, in_=w_in[l, :, COL_V + c * 128:COL_V + (c + 1) * 128].rearrange("(k p) n -> p k n", p=128)),
                   w=[("w", s)], dma=("w", s))
                for j in range(nb):
                    if j % 2 == 0:
                        vpair = next_acc_pair()
                    bank = vpair[j % 2]

                    def fv(e, wv=wv, j=j, bank=bank):
                        last = None
                        for kc in range(KC):
                            last = e.matmul(PS[bank][:, 0:128], lhsT=hT[:, kc, j * 128:(j + 1) * 128], rhs=wv[:, kc, :], start=(kc == 0), stop=(kc == KC - 1))
                        return last
                    op("pe", fv, r=[("w", s)] + [("hT", j, q) for q in range(4)], w=[("ps", bank)])
                    op("dve", lambda e, j=j, c=c, bank=bank: e.tensor_copy(Vt[:, 1 + j, c * 128:(c + 1) * 128], PS[bank][:, 0:128]),
                       r=[("ps", bank), "Vt_all", ("Vt", 0)], w=[("Vt", 1 + j)])

        def branch_b_attn(l, b0, nb, i_lo):
            T = nb * 128
            subs = split2(i_lo * 128, T)
            dn, rg = B_DN, B_RG

            def views(g):
                Q, G = B_Q[g % 2], B_G[g % 2]
                return Q, G, Q.ap.rearrange("p (j t) -> p j t", t=TM), G.ap.rearrange("p (j t) -> p j t", t=TM)

            def chunk_emitters(g):
                Q, G, qv, gv = views(g)
                res = []
                for jj in range(4):
                    def fq(jj=jj):
                        banks_q = inproj_chunk(l, COL_Q + (4 * g + jj) * 128, nb, subs)
                        for si, (t0, n) in enumerate(subs):
                            op("act", lambda e, si=si, t0=t0, n=n: e.activation(
                                out=qv[:, jj, t0:t0 + n], in_=PS[banks_q[si]][:, 0:n], func=AF.Copy, scale=0.125),
                               r=[("ps", banks_q[si])], w=[Q.tag])
                    res.append(fq)
                for jj in range(4):
                    def fz(jj=jj):
                        banks_z = inproj_chunk(l, COL_ZB + (4 * g + jj) * 128, nb, subs)
                        gate_from_psum(banks_z, subs, gv[:, jj, :], G.tag)
                    res.append(fz)
                return res

            def att_p1(g, i):
                Q, G, qv, gv = views(g)
                blk = b0 + i
                mprev = 2 if blk == 2 else 0
                for kb, msk in ((0, mprev), (1, 1)):
                    for half in range(2):
                        r0 = 64 * half
                        bank = 4 + (state["aux"] % 2)
                        state["aux"] += 1
                        P = B_PT[kb * 2 + half]

                        def fs(e, bank=bank, msk=msk, kb=kb, r0=r0):
                            e.matmul(PS[bank][:, :], lhsT=identb[:, :], rhs=mask4[:, msk, :], start=True, stop=False)
                            return e.matmul(q4(PS[bank][:, :]),
                                            lhsT=kdup[r0:r0 + 64, g, (i + kb) * 128:(i + kb + 1) * 128],
                                            rhs=qv[r0:r0 + 64, :, i * 128:(i + 1) * 128], start=False, stop=True)
                        op("pe", fs, r=["identb", "mask4", ("kdup", g), ("kdupx", g), "kdup_halo", "kdup_all", Q.tag],
                           w=[("ps", bank)])
                        op("act", lambda e, bank=bank, P=P: e.activation(out=P.ap, in_=PS[bank][:, :], func=AF.Exp),
                           r=[("ps", bank)], w=[P.tag])

            def att_p2(g, i):
                Q, G, qv, gv = views(g)

                def fo(e):
                    last = None
                    for kb in range(2):
                        for half in range(2):
                            r0 = 64 * half
                            e.matmul(PS[6][r0:r0 + 64, :], lhsT=Vt[:, i + kb, g * 64:(g + 1) * 64], rhs=B_PT[kb * 2 + half].ap,
                                     start=(kb == 0), stop=(kb == 1))
                            last = e.matmul(PS[7][r0:r0 + 64, :], lhsT=onesEO[:, half, :], rhs=B_PT[kb * 2 + half].ap,
                                            start=(kb == 0), stop=(kb == 1))
                    return last
                op("pe", fo, r=[P_.tag for P_ in B_PT] + [("Vt", i), ("Vt", i + 1), "Vt_all", "onesEO"], w=[("ps", 6), ("ps", 7)])
                sk = sinkexp[:, l * 16 + 4 * g:l * 16 + 4 * g + 4]
                skb = bass.AP(sk.tensor, sk.offset, [sk.ap[0], [1, 4], [0, 128]])
                op("dve", lambda e: e.tensor_tensor(out=q4(dn.ap), in0=q4(PS[7][:, :]), in1=skb, op=ALU.add),
                   r=[("ps", 7), "sinkexp"], w=[dn.tag])
                op("dve", lambda e: e.reciprocal(out=dn.ap, in_=dn.ap), r=[dn.tag], w=[dn.tag])
                op("dve", lambda e: e.tensor_tensor(out=q4(rg.ap), in0=gv[:, :, i * 128:(i + 1) * 128], in1=q4(dn.ap), op=ALU.mult),
                   r=[dn.tag, G.tag], w=[rg.tag])
                op("dve", lambda e: e.scalar_tensor_tensor(
                    out=yT[:, 8 + 4 * g:8 + 4 * g + 4, i * 128:(i + 1) * 128],
                    in0=q4(PS[6][:, :]), scalar=0.5, in1=q4(rg.ap), op0=ALU.mult, op1=ALU.mult),
                   r=[("ps", 6), rg.tag], w=[("yT", 8 + 4 * g + jj, i) for jj in range(4)])

            blocks = list(range(i_lo, nb))
            for g in range(5):
                cl = chunk_emitters(g) if g < 4 else [(lambda c=c: c_early(l, nb, c)) for c in range(4)]
                al = []
                if g >= 1:
                    ga = g - 1
                    for k in range(len(blocks) + 1):
                        def stage(k=k, ga=ga):
                            if k >= 1:
                                att_p2(ga, blocks[k - 1])
                            if k < len(blocks):
                                att_p1(ga, blocks[k])
                        al.append(stage)
                for k in range(max(len(cl), len(al))):
                    if k < len(cl):
                        cl[k]()
                    if k < len(al):
                        al[k]()

        def c_early(l, nb, c):
            T = nb * 128
            subs_all = split2(0, T)
            UC = C_UC[c]
            uc = UC.ap
            op("dve", lambda e: e.tensor_copy(uc[:, 0:16], uctail[:, c, :]), r=[("uctail", c), "uctail_all"], w=[UC.tag])
            banks_u = inproj_chunk(l, COL_UC + c * 128, nb, subs_all)
            for si, (t0, n) in enumerate(subs_all):
                op("act", lambda e, si=si, t0=t0, n=n: e.activation(
                    out=uc[:, 16 + t0:16 + t0 + n], in_=PS[banks_u[si]][:, 0:n], func=AF.Copy),
                   r=[("ps", banks_u[si])], w=[UC.tag])
            op("dve", lambda e: e.tensor_copy(uctail[:, c, :], uc[:, T:T + 16]), r=[UC.tag], w=[("uctail", c)])

        def branch_c(l, b0, nb, t_lo):
            T = nb * 128
            subs_all = split2(0, T)
            subs = split2(t_lo, T)
            pwc = C_PW.ap.rearrange("p (g k n) -> p g k n", k=2, n=256)
            op("pool", lambda e: e.dma_start(out=pwc, in_=pool_w[l, :, :].rearrange("(g k p) n -> p g k n", k=2, p=128)),
               w=[C_PW.tag], dma="pwc")
            gates = {}
            for c in range(8):
                gp = c // 2
                win = POOL_WINS[gp]
                UC = C_UC[c]
                uc = UC.ap
                if c >= 4:
                    c_early(l, nb, c)
                src, stag = uc, UC.tag
                nsteps = {2: 1, 4: 2, 8: 3, 16: 4}[win]
                si_ = 0
                for k in range(1, nsteps + 1):
                    sh = 1 << (k - 1)
                    lo = (1 << k) - 1
                    dst, dtag = C_S[si_].ap, C_S[si_].tag
                    op("dve", lambda e, src=src, dst=dst, sh=sh, lo=lo: e.tensor_tensor(
                        out=dst[:, lo:16 + T], in0=src[:, lo:16 + T], in1=src[:, lo - sh:16 + T - sh], op=ALU.add),
                       r=[stag], w=[dtag])
                    src, stag = dst, dtag
                    si_ ^= 1
                MX = C_MX[c]
                op("dve", lambda e, src=src, uc=uc, MX=MX, win=win: e.scalar_tensor_tensor(
                    out=MX.ap[:, 0:T], in0=src[:, 16:16 + T], scalar=1.0 / win, in1=uc[:, 16:16 + T], op0=ALU.mult, op1=ALU.subtract),
                   r=[stag, UC.tag], w=[MX.tag])
                if b0 <= 2 < b0 + nb:
                    off = (2 - b0) * 128
                    iv = cst[:, 512 + gp * 16:512 + gp * 16 + 16]
                    tmpd, tdtag = C_S[si_].ap, C_S[si_].tag
                    op("dve", lambda e, src=src, off=off, iv=iv, tmpd=tmpd: e.tensor_tensor(
                        out=tmpd[:, 0:16], in0=src[:, 16 + off:32 + off], in1=iv, op=ALU.mult),
                       r=[stag, "cst"], w=[tdtag])
                    op("dve", lambda e, tmpd=tmpd, uc=uc, MX=MX, off=off: e.tensor_tensor(
                        out=MX.ap[:, off:off + 16], in0=tmpd[:, 0:16], in1=uc[:, 16 + off:32 + off], op=ALU.subtract),
                       r=[tdtag, UC.tag], w=[MX.tag])
                banks_z = inproj_chunk(l, COL_ZC + c * 128, nb, subs)
                gt = next_tmp()
                gate_from_psum(banks_z, subs, gt.ap, gt.tag)
                gates[c] = gt
                if c % 2 == 1:
                    for dch in range(2):
                        cc = 2 * gp + dch
                        bk = (4, 5) if dch == 0 else (6, 7)

                        def fpl(e, gp=gp, dch=dch, bk=bk):
                            last = None
                            for k in range(2):
                                for si, (t0, n) in enumerate(subs):
                                    last = e.matmul(PS[bk[si]][:, 0:n], lhsT=pwc[:, gp, k, dch * 128:(dch + 1) * 128],
                                                    rhs=C_MX[2 * gp + k].ap[:, t0:t0 + n], start=(k == 0), stop=(k == 1))
                            return last
                        op("pe", fpl, r=[C_PW.tag, C_MX[2 * gp].tag, C_MX[2 * gp + 1].tag], w=[("ps", bk[0]), ("ps", bk[1])])
                        gt_ = gates[cc]
                        for si, (t0, n) in enumerate(subs):
                            op("dve", lambda e, si=si, t0=t0, n=n, bk=bk, gt_=gt_, cc=cc: e.scalar_tensor_tensor(
                                out=yT[:, 24 + cc, t0:t0 + n], in0=PS[bk[si]][:, 0:n], scalar=SMP_PSC(l, cc), in1=gt_.ap[:, t0:t0 + n],
                                op0=ALU.mult, op1=ALU.mult),
                               r=[("ps", bk[si]), gt_.tag, "smp"], w=[("yT", 24 + cc, j) for j in range(nb)])

        def out_proj(l, xsrc, xdst, b0, nb, j_lo):
            nj = nb - j_lo
            WO = [AV(0, 8192, BF16), AV(4096, 8192, BF16)]
            XR = [AV(8192, tb * 256), AV(8192 + tb * 256, tb * 256)]
            XO = [AV(8192 + 2 * tb * 256, tb * 256), AV(8192 + 3 * tb * 256, tb * 256)]
            assert 8192 + 4 * tb * 256 <= ARENA_F32
            r0, r1 = (b0 + j_lo) * 128, (b0 + nb) * 128

            def prefetch(cb):
                W = WO[cb % 2]
                wo = W.ap.rearrange("p (k n) -> p k n", n=256)
                op("pool", lambda e, wo=wo, cb=cb: e.dma_start(out=wo, in_=w_out[l, :, cb * 256:(cb + 1) * 256].rearrange("(k p) n -> p k n", p=128)),
                   w=[W.tag], dma=("wo", cb % 2))
                X = XR[cb % 2]
                xr = X.ap[:, 0:nj * 256].rearrange("p (j n) -> p j n", n=256)
                op("sp", lambda e, xr=xr, cb=cb: e.dma_start(out=xr, in_=xsrc[r0:r1, cb * 256:(cb + 1) * 256].rearrange("(j p) n -> p j n", p=128)),
                   r=[("x", id(xsrc), b0 + j, cb) for j in range(j_lo, nb)], w=[X.tag], dma=("xr", cb % 2))
            prefetch(0)
            gi = 0
            for cb in range(16):
                if cb + 1 < 16:
                    prefetch(cb + 1)
                W, X, O = WO[cb % 2], XR[cb % 2], XO[cb % 2]
                wo = W.ap.rearrange("p (k n) -> p k n", n=256)
                xr = X.ap[:, 0:nj * 256].rearrange("p (j n) -> p j n", n=256)
                xo = O.ap[:, 0:nj * 256].rearrange("p (j n) -> p j n", n=256)
                for j in range(j_lo, nb):
                    bank = gi % 4
                    gi += 1

                    def fop(e, wo=wo, j=j, bank=bank):
                        last = None
                        for kc in range(KC):
                            last = e.matmul(PS[bank][:, 0:256], lhsT=yT[:, kc, j * 128:(j + 1) * 128], rhs=wo[:, kc, :], start=(kc == 0), stop=(kc == KC - 1))
                        return last
                    op("pe", fop, r=[W.tag] + [("yT", c, j) for c in range(KC)], w=[("ps", bank)])
                    op("dve", lambda e, xr=xr, xo=xo, bank=bank, j=j: e.tensor_tensor(out=xo[:, j - j_lo, :], in0=PS[bank][:, 0:256], in1=xr[:, j - j_lo, :], op=ALU.add),
                       r=[("ps", bank), X.tag], w=[O.tag])
                op("act", lambda e, xo=xo, cb=cb: e.dma_start(out=xdst[r0:r1, cb * 256:(cb + 1) * 256].rearrange("(j p) n -> p j n", p=128), in_=xo),
                   r=[O.tag], w=[("x", id(xdst), b0 + j, cb) for j in range(j_lo, nb)], dma=("xo", cb % 2))

        def final_phase(xsrc):
            XT = [AV(0, 4096), AV(4096, 4096), AV(8192, 4096)]
            JK = AV(12288, 2048, BF16)
            gbf = Wt.bitcast(F32)[:, 0:2, :, :].rearrange("p a k n -> p (a k n)")
            GBT = [("w", 0), ("w", 1)]
            op("sp", lambda e: e.dma_start(out=gbf, in_=norm_g[2:3, :].partition_broadcast(128)), w=GBT, dma="gbf")
            for j in range(n_real_blk):
                blk = 2 + j
                xs = j % 3
                xt = XT[xs]
                op("sp", lambda e, xt=xt, blk=blk: e.dma_start(out=xt.ap, in_=xsrc[blk * 128:(blk + 1) * 128, :]),
                   r=[("x", id(xsrc), blk, cb) for cb in range(16)], w=[xt.tag], dma=("xtf", xs))
                ssA = stat[:, 3 * tb + 0:3 * tb + 1]
                ssB = stat[:, 3 * tb + 1:3 * tb + 2]
                ss = stat[:, 3 * tb + 2 + xs:3 * tb + 3 + xs]
                op("act", lambda e, xt=xt, ssA=ssA: e.activation(out=JK.ap, in_=xt.ap[:, 0:2048], func=AF.Square, accum_out=ssA),
                   r=[xt.tag], w=[JK.tag, "ssfA"])
                op("act", lambda e, xt=xt, ssB=ssB: e.activation(out=JK.ap, in_=xt.ap[:, 2048:4096], func=AF.Square, accum_out=ssB),
                   r=[xt.tag], w=[JK.tag, "ssfB"])
                op("dve", lambda e, ss=ss, ssA=ssA, ssB=ssB: e.tensor_tensor(out=ss, in0=ssA, in1=ssB, op=ALU.add),
                   r=["ssfA", "ssfB"], w=[("ssf", xs)])
                op("act", lambda e, ss=ss: e.activation(out=ss, in_=ss, func=AF.Sqrt, scale=1.0 / D, bias=epsN[:, 0:1]),
                   r=[("ssf", xs), "epsN"], w=[("ssf", xs)])
                op("dve", lambda e, ss=ss: e.reciprocal(out=ss, in_=ss), r=[("ssf", xs)], w=[("ssf", xs)])
                op("dve", lambda e, xt=xt, ss=ss: e.scalar_tensor_tensor(out=xt.ap, in0=xt.ap, scalar=ss, in1=gbf, op0=ALU.mult, op1=ALU.mult),
                   r=[xt.tag, ("ssf", xs)] + GBT, w=[xt.tag])
                op("pool", lambda e, xt=xt, j=j: e.dma_start(out=out[j * 128:(j + 1) * 128, :], in_=xt.ap),
                   r=[xt.tag], w=[("out", j)], dma=("xto", xs))

        def tiles(start):
            res = []
            b = start
            while b < NBE:
                n = min(tb, NBE - b)
                res.append((b, n))
                b += n
            return res

        srcs = [x_ext, x1, x2]
        for l in range(n_layers):
            if l > 0:
                KD = [("kdup", h) for h in range(4)] + [("kdupx", h) for h in range(4)]
                op("dve", lambda e: e.memset(kdup[:, :, :], 0.0), w=KD + ["kdup_halo", "kdup_all"])
                op("dve", lambda e: e.memset(Vt[:, :, :], 0.0), w=[("Vt", j) for j in range(tb + 1)] + ["Vt_all"])
                op("dve", lambda e: e.memset(utail[:, :, :], 0.0), w=[("utail", c) for c in range(8)] + ["utail_all"])
                op("dve", lambda e: e.memset(uctail[:, :, :], 0.0), w=[("uctail", c) for c in range(8)] + ["uctail_all"])
            prevT = tb * 128
            for ti, (b0, nb) in enumerate(tiles(l)):
                j_lo = 1 if ti == 0 else 0
                t_lo = 128 * j_lo
                norm_phase(srcs[l], l, b0, nb)
                last_stats = branch_a1(l, nb, t_lo)
                branch_b_kv(l, nb, prevT)
                while last_stats:
                    last_stats.pop(0)()
                branch_a2(l, nb, t_lo)
                branch_b_attn(l, b0, nb, j_lo)
                branch_c(l, b0, nb, t_lo)
                out_proj(l, srcs[l], srcs[l + 1], b0, nb, j_lo)
                prevT = nb * 128
        final_phase(srcs[n_layers])

        _emit(nc, ops, es)
    return nc


def _host_consts(core_at_seq_start):
    c = np.zeros((128, 576), np.float32)
    c[:, 0:128] = np.eye(128, dtype=np.float32)
    j = np.arange(128)[:, None]
    i = np.arange(128)[None, :]
    maskP = np.where(j >= i, 0.0, NEG).astype(np.float32)
    maskC = np.where(j <= i, 0.0, NEG).astype(np.float32)
    c[:, 128:256] = maskP
    c[:, 256:384] = maskC
    c[:, 384:512] = NEG if core_at_seq_start else maskP
    for g, win in enumerate(POOL_WINS):
        pos = np.arange(1, 17, dtype=np.float32)
        div = np.minimum(pos, float(win)) if core_at_seq_start else np.full(16, float(win), np.float32)
        c[:, 512 + g * 16:512 + (g + 1) * 16] = (1.0 / div)[None, :]
    return c


def _host_small(conv_dw, conv_ln_g, conv_ln_b, pool_scale, attn_sinks):
    s = np.zeros((128, 576), np.float32)
    dw = np.transpose(conv_dw.reshape(2, 31, 8, 128), (3, 0, 2, 1))
    s[:, 0:496] = dw.reshape(128, 496)
    s[:, 496:512] = np.transpose(conv_ln_g.reshape(2, 8, 128), (2, 0, 1)).reshape(128, 16)
    s[:, 512:528] = np.transpose(conv_ln_b.reshape(2, 8, 128), (2, 0, 1)).reshape(128, 16)
    s[:, 528:544] = np.transpose(pool_scale.reshape(2, 8, 128), (2, 0, 1)).reshape(128, 16)
    sk = attn_sinks.reshape(2, 16, 2)
    s[0:64, 544:576] = sk[:, :, 0].reshape(1, 32)
    s[64:128, 544:576] = sk[:, :, 1].reshape(1, 32)
    return s


_NC_CACHE = {}


def _run(inputs, n_cores, tok_per_core, seq_len, tb=6, n_layers=2):
    x = np.asarray(inputs["x"], np.float32)
    B, S, _ = x.shape
    nrb = tok_per_core // 128
    key = (nrb, tb, n_layers)
    if key not in _NC_CACHE:
        _NC_CACHE[key] = build_program(nrb, tb, n_layers)
    nc = _NC_CACHE[key]
    xf = x.reshape(B * S, D)
    f32 = lambda a: np.ascontiguousarray(np.asarray(a, np.float32))
    w_in = f32(inputs["w_in"])
    conv_pw = f32(inputs["conv_pw"])
    pool_w = f32(inputs["pool_w"]).reshape(2, 1024, 256)
    w_out = f32(inputs["w_out"])
    norm_g = np.concatenate([f32(inputs["norm_g"]), f32(inputs["final_norm_g"])[None, :]], axis=0)
    small = _host_small(f32(inputs["conv_dw"]), f32(inputs["conv_ln_g"]), f32(inputs["conv_ln_b"]),
                        f32(inputs["pool_scale"]), f32(inputs["attn_sinks"]))
    in_maps = []
    for c in range(n_cores):
        t0 = c * tok_per_core
        at_start = (t0 % seq_len) == 0
        xe = np.zeros((256 + tok_per_core, D), np.float32)
        xe[256:] = xf[t0:t0 + tok_per_core]
        if not at_start:
            xe[0:256] = xf[t0 - 256:t0]
        in_maps.append({"x_ext": xe, "w_in": w_in, "conv_pw": conv_pw, "pool_w": pool_w, "w_out": w_out,
                        "norm_g": norm_g, "smallp": small, "cpk": _host_consts(at_start)})
    res = run_bass_kernel_spmd(nc, in_maps, core_ids=list(range(n_cores)))
    outs = [np.asarray(r["out"], np.float32) for r in res.results]
    return np.concatenate(outs, axis=0)


def kernel(x, norm_g, w_in, conv_dw, conv_ln_g, conv_ln_b, conv_pw, attn_sinks, pool_w, pool_scale, w_out, final_norm_g):
    inputs = dict(x=x, norm_g=norm_g, w_in=w_in, conv_dw=conv_dw, conv_ln_g=conv_ln_g, conv_ln_b=conv_ln_b,
                  conv_pw=conv_pw, attn_sinks=attn_sinks, pool_w=pool_w, pool_scale=pool_scale, w_out=w_out,
                  final_norm_g=final_norm_g)
    xs = np.asarray(x)
    B, S, _ = xs.shape
    o = _run(inputs, 8, (B * S) // 8, S)
    return o.reshape(B, S, D).astype(np.float32)
```

```python
import numpy as np
from contextlib import ExitStack
import concourse.bass as bass
import concourse.mybir as mybir
from concourse.bass_utils import run_bass_kernel_spmd

F32 = mybir.dt.float32
BF16 = mybir.dt.bfloat16
AF = mybir.ActivationFunctionType
ALU = mybir.AluOpType

D = 4096
INW = 9728
KC = 32
NEG = -30000.0
COL_A, COL_B, COL_ZA, COL_Q, COL_K, COL_V, COL_ZB, COL_UC, COL_ZC = 0, 1024, 2048, 3072, 5120, 5376, 5632, 7680, 8704
NORM_EPS = 1e-5
LN_EPS = 1e-5
POOL_WINS = (2, 4, 8, 16)


class _Op:
    __slots__ = ("eng", "fn", "r", "w", "dma", "deps", "idx", "semval", "has_dep")

    def __init__(self, eng, fn, r=(), w=(), dma=None):
        self.eng, self.fn, self.r, self.w, self.dma = eng, fn, tuple(r), tuple(w), dma
        self.deps = None
        self.semval = 0
        self.has_dep = False


def _is_iv(t):
    return isinstance(t, tuple) and len(t) == 3 and t[0] == "@"


def _analyze(ops):
    last_w, readers, last_dma = {}, {}, {}
    ivals = []
    for i, op in enumerate(ops):
        op.idx = i
        deps = set()
        for x in op.r:
            if _is_iv(x):
                for a in ivals:
                    if a[3] and a[0] < x[2] and x[1] < a[1]:
                        deps.add(a[2])
            elif x in last_w:
                deps.add(last_w[x])
        for x in op.w:
            if _is_iv(x):
                for a in ivals:
                    if a[0] < x[2] and x[1] < a[1]:
                        deps.add(a[2])
            else:
                if x in last_w:
                    deps.add(last_w[x])
                deps.update(readers.get(x, ()))
        if op.dma is not None:
            if op.dma in last_dma:
                deps.add(last_dma[op.dma])
            last_dma[op.dma] = i
        deps.discard(i)
        op.deps = deps
        for x in op.w:
            if _is_iv(x):
                ivals = [a for a in ivals if not (x[1] <= a[0] and a[1] <= x[2])]
                ivals.append([x[1], x[2], i, True])
            else:
                last_w[x] = i
                readers[x] = []
        for x in op.r:
            if _is_iv(x):
                if op.dma is None:
                    ivals = [a for a in ivals if not ((not a[3]) and ops[a[2]].dma is None and ops[a[2]].eng == op.eng
                                                      and x[1] <= a[0] and a[1] <= x[2])]
                ivals.append([x[1], x[2], i, False])
                continue
            lst = readers.setdefault(x, [])
            if op.dma is None:
                for k in range(len(lst) - 1, -1, -1):
                    o = ops[lst[k]]
                    if o.dma is None and o.eng == op.eng:
                        del lst[k]
            lst.append(i)
    for op in ops:
        for d in op.deps:
            dop = ops[d]
            if dop.dma is None and dop.eng == "pe" and op.eng == "pe" and op.dma is None:
                continue
            dop.has_dep = True
    cnt, dcnt = {}, {}
    for op in ops:
        if op.dma is not None:
            dcnt[op.dma] = dcnt.get(op.dma, 0) + 16
            op.semval = dcnt[op.dma]
        elif op.has_dep:
            cnt[op.eng] = cnt.get(op.eng, 0) + 1
            op.semval = cnt[op.eng]
    return sorted(dcnt.keys(), key=str)


def _emit(nc, ops, es):
    dma_keys = _analyze(ops)
    esem = {e: es.enter_context(nc.semaphore("prog_" + e)) for e in ("pe", "act", "dve", "pool", "sp")}
    dsem = {k: es.enter_context(nc.semaphore("dma%d" % i)) for i, k in enumerate(dma_keys)}

    def run_engine(name, eng):
        waited = {}
        for op in ops:
            if op.eng != name:
                continue
            waits = {}
            for d in op.deps:
                dop = ops[d]
                if dop.dma is not None:
                    key = ("d", dop.dma)
                else:
                    if dop.eng == "pe" and name == "pe" and op.dma is None:
                        continue
                    key = ("e", dop.eng)
                if waits.get(key, 0) < dop.semval:
                    waits[key] = dop.semval
            for key, val in waits.items():
                if waited.get(key, 0) >= val:
                    continue
                waited[key] = val
                eng.wait_ge(dsem[key[1]] if key[0] == "d" else esem[key[1]], val)
            ins = op.fn(eng)
            if op.dma is not None:
                ins.then_inc(dsem[op.dma], 16)
            elif op.has_dep:
                ins.then_inc(esem[name], 1)
        fin = {}
        for op in ops:
            if op.eng == name and op.dma is not None:
                fin[op.dma] = op.semval
        for k, v in fin.items():
            if waited.get(("d", k), 0) < v:
                eng.wait_ge(dsem[k], v)

    with nc.Block() as blk:
        blk.tensor(lambda e: run_engine("pe", e))
        blk.scalar(lambda e: run_engine("act", e))
        blk.vector(lambda e: run_engine("dve", e))
        blk.gpsimd(lambda e: run_engine("pool", e))
        blk.sync(lambda e: run_engine("sp", e))


def build_program(n_real_blk=16, tb=6, n_layers=2):
    NBE = n_real_blk + 2
    TMAX = tb * 128
    nc = bass.Bass("TRN2", target_bir_lowering=False)
    x_ext = nc.dram_tensor("x_ext", [NBE * 128, D], F32, kind="ExternalInput").ap()
    w_in = nc.dram_tensor("w_in", [2, D, INW], F32, kind="ExternalInput").ap()
    conv_pw = nc.dram_tensor("conv_pw", [2, 1024, 1024], F32, kind="ExternalInput").ap()
    pool_w = nc.dram_tensor("pool_w", [2, 1024, 256], F32, kind="ExternalInput").ap()
    w_out = nc.dram_tensor("w_out", [2, D, D], F32, kind="ExternalInput").ap()
    norm_g = nc.dram_tensor("norm_g", [3, D], F32, kind="ExternalInput").ap()
    smallp = nc.dram_tensor("smallp", [128, 576], F32, kind="ExternalInput").ap()
    cpk = nc.dram_tensor("cpk", [128, 576], F32, kind="ExternalInput").ap()
    out = nc.dram_tensor("out", [n_real_blk * 128, D], F32, kind="ExternalOutput").ap()
    x1 = nc.dram_tensor("x1s", [NBE * 128, D], F32, kind="Internal").ap()
    x2 = nc.dram_tensor("x2s", [NBE * 128, D], F32, kind="Internal").ap()

    es = ExitStack()
    with es:
        def sb(name, shape, dt):
            return es.enter_context(nc.sbuf_tensor(name, shape, dt))

        hT = sb("hT", [128, KC, TMAX], BF16)
        yT = sb("yT", [128, KC, TMAX], BF16)
        NWS = 3
        Wt = sb("Wt", [128, NWS, KC, 128], BF16)
        cst = sb("cst", [128, 576], F32)
        smp = sb("smp", [128, 576], F32)
        identb = sb("identb", [128, 128], BF16)
        mask4 = sb("mask4", [128, 3, 512], BF16)
        onesEO = sb("onesEO", [128, 2, 64], BF16)
        onesf = sb("onesf", [128, 128], F32)
        sinkexp = sb("sinkexp", [128, 32], F32)
        kdup = sb("kdup", [128, 4, 128 + TMAX], BF16)
        Vt = sb("Vt", [128, tb + 1, 256], BF16)
        utail = sb("utail", [128, 8, 30], F32)
        uctail = sb("uctail", [128, 8, 16], F32)
        stat = sb("stat", [128, 4 * tb + 8], F32)
        ARENA_F32 = 15616
        arena = sb("arena", [128, ARENA_F32], F32)
        arena_b = arena[:, :].bitcast(BF16)

        def af(off, n):
            assert off + n <= ARENA_F32, (off, n)
            return arena[:, off:off + n]

        def ab(off, n):
            assert off * 2 + n <= 2 * ARENA_F32, (off, n)
            return arena_b[:, 2 * off:2 * off + n]

        PS = [es.enter_context(nc.psum_tensor("ps%d" % i, [128, 512], F32)) for i in range(8)]

        ops = []

        def op(eng, fn, r=(), w=(), dma=None):
            ops.append(_Op(eng, fn, r, w, dma))

        class AV:
            def __init__(self, off, n, dt=F32):
                self.ap = af(off, n) if dt == F32 else ab(off, n)
                self.tag = ("@", off, off + (n if dt == F32 else (n + 1) // 2))

        op("sp", lambda e: e.dma_start(out=cst[:, :], in_=cpk), w=["cst"], dma="cst")
        op("sp", lambda e: e.dma_start(out=smp[:, :], in_=smallp), w=["smp"], dma="smp")
        op("dve", lambda e: e.tensor_copy(identb[:, :], cst[:, 0:128]), r=["cst"], w=["identb"])

        def mk_mask(m):
            def f(e):
                src = cst[:, 128 * (m + 1):128 * (m + 2)]
                bc = bass.AP(src.tensor, src.offset, [src.ap[0], [0, 4], [1, 128]])
                return e.tensor_copy(mask4[:, m, :].rearrange("p (j q) -> p j q", q=128), bc)
            op("dve", f, r=["cst"], w=["mask4"])
        for m in range(3):
            mk_mask(m)
        op("dve", lambda e: e.memset(onesEO[:, :, :], 1.0), w=["onesEO"])
        op("dve", lambda e: e.memset(onesf[:, :], 1.0), w=["onesf"])
        op("dve", lambda e: e.memset(kdup[:, :, :], 0.0), w=["kdup_all"])
        op("dve", lambda e: e.memset(Vt[:, :, :], 0.0), w=["Vt_all"])
        op("dve", lambda e: e.memset(utail[:, :, :], 0.0), w=["utail_all"])
        op("dve", lambda e: e.memset(uctail[:, :, :], 0.0), w=["uctail_all"])
        op("dve", lambda e: e.tensor_scalar(out=smp[:, 0:496], in0=smp[:, 0:496], scalar1=0.5, scalar2=None, op0=ALU.mult),
           r=["smp"], w=["smp"])
        op("dve", lambda e: e.tensor_scalar(out=smp[:, 528:544], in0=smp[:, 528:544], scalar1=0.5, scalar2=None, op0=ALU.mult),
           r=["smp"], w=["smp"])
        op("act", lambda e: e.activation(out=sinkexp[:, :], in_=smp[:, 544:576], func=AF.Exp), r=["smp"], w=["sinkexp"])
        epsN = sb("epsN", [128, 1], F32)
        epsL = sb("epsL", [128, 1], F32)
        op("dve", lambda e: e.memset(epsN[:, :], NORM_EPS), w=["epsN"])
        op("dve", lambda e: e.memset(epsL[:, :], LN_EPS), w=["epsL"])

        SMP_DW = lambda l, c, j: smp[:, (l * 8 + c) * 31 + j:(l * 8 + c) * 31 + j + 1]
        SMP_LNG = lambda l, c: smp[:, 496 + l * 8 + c:496 + l * 8 + c + 1]
        SMP_LNB = lambda l, c: smp[:, 512 + l * 8 + c:512 + l * 8 + c + 1]
        SMP_PSC = lambda l, c: smp[:, 528 + l * 8 + c:528 + l * 8 + c + 1]

        state = {"ws": 0, "acc": 0, "aux": 0, "tmp": 0}
        TM = TMAX

        def next_w():
            s = state["ws"]
            state["ws"] = (s + 1) % NWS
            return s

        def next_acc_pair():
            a = state["acc"]
            state["acc"] = (a + 1) % 2
            return (2 * a, 2 * a + 1)

        def split2(lo, hi):
            h = (hi - lo) // 2
            return [(lo, h), (lo + h, hi - lo - h)]

        def q4(ap):
            return ap.rearrange("p (j q) -> p j q", q=128)

        TMPV = [AV(i * TM, TM) for i in range(4)]

        def next_tmp():
            i = state["tmp"]
            state["tmp"] = (i + 1) % 4
            return TMPV[i]
        BR0 = 4 * TM

        def norm_phase(xsrc, gidx, b0, nb):
            XH = [AV(i * 2048, 2048) for i in range(4)]
            HNH = [AV(8192, 2048, BF16), AV(9216, 2048, BF16)]
            GBV = AV(10240, 4096)
            JK = AV(14336, 2048, BF16)
            def load_gb():
                op("sp", lambda e: e.dma_start(out=GBV.ap, in_=norm_g[gidx:gidx + 1, :].partition_broadcast(128)),
                   w=[GBV.tag], dma="gb")

            def stage_stats(j):
                blk = b0 + j
                for h in range(2):
                    u = (2 * j + h) % 4
                    xt = XH[u]
                    op("sp", lambda e, xt=xt, blk=blk, h=h: e.dma_start(out=xt.ap, in_=xsrc[blk * 128:(blk + 1) * 128, h * 2048:(h + 1) * 2048]),
                       r=[("x", id(xsrc), blk, cb) for cb in range(8 * h, 8 * h + 8)], w=[xt.tag], dma=("xt", u))
                    ssh = stat[:, 3 * j + h:3 * j + h + 1]
                    op("act", lambda e, xt=xt, ssh=ssh: e.activation(out=JK.ap, in_=xt.ap, func=AF.Square, accum_out=ssh),
                       r=[xt.tag], w=[JK.tag, ("ssh", j, h)])
                ss = stat[:, 3 * j + 2:3 * j + 3]
                op("dve", lambda e, ss=ss, j=j: e.tensor_tensor(out=ss, in0=stat[:, 3 * j:3 * j + 1], in1=stat[:, 3 * j + 1:3 * j + 2], op=ALU.add),
                   r=[("ssh", j, 0), ("ssh", j, 1)], w=[("ss", j)])
                op("act", lambda e, ss=ss: e.activation(out=ss, in_=ss, func=AF.Sqrt, scale=1.0 / D, bias=epsN[:, 0:1]),
                   r=[("ss", j), "epsN"], w=[("ss", j)])
                op("dve", lambda e, ss=ss: e.reciprocal(out=ss, in_=ss), r=[("ss", j)], w=[("ss", j)])

            def stage_norm_tr(j):
                ss = stat[:, 3 * j + 2:3 * j + 3]
                for h in range(2):
                    xt = XH[(2 * j + h) % 4]
                    HN = HNH[h]
                    op("dve", lambda e, xt=xt, ss=ss, HN=HN, h=h: e.scalar_tensor_tensor(
                        out=HN.ap, in0=xt.ap, scalar=ss, in1=GBV.ap[:, h * 2048:(h + 1) * 2048], op0=ALU.mult, op1=ALU.mult),
                       r=[xt.tag, ("ss", j), GBV.tag], w=[HN.tag])
                    for qq in range(2):
                        q = 2 * h + qq
                        bank = 4 + (state["aux"] % 4)
                        state["aux"] += 1
                        pst = PS[bank][:, :].bitcast(BF16)

                        def ftr(e, qq=qq, pst=pst, HN=HN):
                            last = None
                            for i in range(8):
                                c = qq * 8 + i
                                last = e.transpose(pst[:, i * 128:(i + 1) * 128], HN.ap[:, c * 128:(c + 1) * 128], identb[:, :])
                            return last
                        op("pe", ftr, r=[HN.tag, "identb"], w=[("ps", bank)])
                        if qq == 0:
                            op("act", lambda e, q=q, pst=pst, j=j: e.activation(
                                out=hT[:, q * 8:(q + 1) * 8, j * 128:(j + 1) * 128],
                                in_=pst.rearrange("p (c t) -> p c t", t=128), func=AF.Copy),
                               r=[("ps", bank)], w=[("hT", j, q)])
                        else:
                            op("dve", lambda e, q=q, pst=pst, j=j: e.tensor_copy(
                                hT[:, q * 8:(q + 1) * 8, j * 128:(j + 1) * 128], pst.rearrange("p (c t) -> p c t", t=128)),
                               r=[("ps", bank)], w=[("hT", j, q)])

            stage_stats(0)
            load_gb()
            for j in range(nb):
                if j + 1 < nb:
                    stage_stats(j + 1)
                stage_norm_tr(j)

        def inproj_chunk(l, col0, nb, subs):
            s = next_w()
            wv = Wt[:, s, :, :]
            op("pool", lambda e: e.dma_start(out=wv, in_=w_in[l, :, col0:col0 + 128].rearrange("(k p) n -> p k n", p=128)),
               w=[("w", s)], dma=("w", s))
            banks = next_acc_pair()

            def f(e):
                last = None
                for kc in range(KC):
                    for si, (t0, n) in enumerate(subs):
                        last = e.matmul(PS[banks[si]][:, 0:n], lhsT=wv[:, kc, :], rhs=hT[:, kc, t0:t0 + n],
                                        start=(kc == 0), stop=(kc == KC - 1))
                return last
            op("pe", f, r=[("w", s)] + [("hT", j, q) for j in range(nb) for q in range(4)],
               w=[("ps", banks[0]), ("ps", banks[1])])
            return banks

        def gate_from_psum(banks, subs, dst, dst_tag):
            dst_tags = dst_tag if isinstance(dst_tag, list) else [dst_tag]
            tmp = next_tmp()
            for si, (t0, n) in enumerate(subs):
                op("act", lambda e, si=si, t0=t0, n=n: e.activation(out=tmp.ap[:, t0:t0 + n], in_=PS[banks[si]][:, 0:n], func=AF.Tanh, scale=0.5),
                   r=[("ps", banks[si])], w=[tmp.tag])
            for si, (t0, n) in enumerate(subs):
                op("dve", lambda e, si=si, t0=t0, n=n: e.scalar_tensor_tensor(
                    out=dst[:, t0:t0 + n], in0=tmp.ap[:, t0:t0 + n], scalar=1.0, in1=PS[banks[si]][:, 0:n], op0=ALU.add, op1=ALU.mult),
                   r=[tmp.tag, ("ps", banks[si])], w=dst_tags)

        oA = BR0
        A_U = [AV(oA + i * (TM + 30), TM + 30) for i in range(3)]; oA += 3 * (TM + 30)
        A_YC = [AV(oA + c * TM, TM) for c in range(8)]
        A_S = [AV(oA + c * TM, TM, BF16) for c in range(8)]; oA += 8 * TM
        A_MU = AV(oA, TM); oA += TM
        A_RS = AV(oA, TM); oA += TM
        A_PW = [AV(oA, 1024, BF16), AV(oA + 512, 1024, BF16)]; oA += 1024
        assert oA <= ARENA_F32, oA
        oB = BR0
        B_Q = [AV(oB, 4 * TM, BF16), AV(oB + 2 * TM, 4 * TM, BF16)]; oB += 4 * TM
        B_G = [AV(oB, 4 * TM), AV(oB + 4 * TM, 4 * TM)]; oB += 8 * TM
        B_PT = [AV(oB + 256 * i, 512, BF16) for i in range(4)]; oB += 1024
        B_DN = AV(oB, 512); oB += 512
        B_RG = AV(oB, 512); oB += 512
        assert oB <= ARENA_F32, oB
        UCW = TM + 16
        _uc_off = [BR0] + [BR0 + 4 * TM + k * UCW for k in range(3)]
        oC = BR0 + 4 * TM + 3 * UCW
        _uc_off += [oC + k * UCW for k in range(4)]; oC += 4 * UCW
        C_UC = [AV(o_, UCW) for o_ in _uc_off]
        C_PW = AV(BR0 + UCW, 2048, BF16)
        _s0 = BR0 + UCW + 1024
        assert _s0 + UCW <= BR0 + 4 * TM
        C_MX = [AV(oC + c * (TM // 2), TM, BF16) for c in range(8)]; oC += 4 * TM
        C_S = [AV(_s0, UCW), AV(oC, UCW)]; oC += UCW
        assert oC <= ARENA_F32, oC

        G_YT = [yT[:, 2 * m:2 * m + 2, :].rearrange("p c t -> p (c t)").bitcast(F32) for m in range(4)]
        G_YT_TAGS = [[("yT", 2 * m + k, j) for k in range(2) for j in range(tb)] for m in range(4)]

        def branch_a1(l, nb, t_lo):
            T = nb * 128
            subs = split2(0, T)
            for c in range(8):
                U = A_U[c % 3]
                op("dve", lambda e, U=U, c=c: e.tensor_copy(U.ap[:, 0:30], utail[:, c, :]),
                   r=[("utail", c), "utail_all"], w=[U.tag])
                banks_b = inproj_chunk(l, COL_B + c * 128, nb, subs)
                tmp = next_tmp()
                for si, (t0, n) in enumerate(subs):
                    op("act", lambda e, si=si, t0=t0, n=n, tmp=tmp, banks_b=banks_b: e.activation(
                        out=tmp.ap[:, t0:t0 + n], in_=PS[banks_b[si]][:, 0:n], func=AF.Tanh, scale=0.5),
                       r=[("ps", banks_b[si])], w=[tmp.tag])
                banks_a = inproj_chunk(l, COL_A + c * 128, nb, subs)
                for si, (t0, n) in enumerate(subs):
                    op("dve", lambda e, si=si, t0=t0, n=n, tmp=tmp, U=U, banks_a=banks_a: e.scalar_tensor_tensor(
                        out=U.ap[:, 30 + t0:30 + t0 + n], in0=tmp.ap[:, t0:t0 + n], scalar=1.0, in1=PS[banks_a[si]][:, 0:n],
                        op0=ALU.add, op1=ALU.mult),
                       r=[tmp.tag, ("ps", banks_a[si])], w=[U.tag])
                Y = A_YC[c]
                op("dve", lambda e, U=U, Y=Y, c=c: e.tensor_scalar(out=Y.ap[:, t_lo:T], in0=U.ap[:, t_lo:T], scalar1=SMP_DW(l, c, 0), scalar2=None, op0=ALU.mult),
                   r=[U.tag, "smp"], w=[Y.tag])
                for j in range(1, 31):
                    if j == 8 and c % 2 == 1:
                        m = c // 2
                        subs_g = split2(t_lo, T)
                        banks_z = inproj_chunk(l, COL_ZA + m * 128, nb, subs_g)
                        gate_from_psum(banks_z, subs_g, G_YT[m], G_YT_TAGS[m])
                    op("dve", lambda e, U=U, Y=Y, c=c, j=j: e.scalar_tensor_tensor(
                        out=Y.ap[:, t_lo:T], in0=U.ap[:, t_lo + j:j + T], scalar=SMP_DW(l, c, j), in1=Y.ap[:, t_lo:T], op0=ALU.mult, op1=ALU.add),
                       r=[U.tag, "smp", Y.tag], w=[Y.tag])
                op("dve", lambda e, U=U, c=c: e.tensor_copy(utail[:, c, :], U.ap[:, T:T + 30]), r=[U.tag], w=[("utail", c)])

        def branch_a2(l, nb, t_lo):
            T = nb * 128
            subs = split2(t_lo, T)
            for c in range(8):
                Y = A_YC[c]
                tmp = next_tmp()
                op("act", lambda e, Y=Y, tmp=tmp: e.activation(out=tmp.ap[:, t_lo:T], in_=Y.ap[:, t_lo:T], func=AF.Square),
                   r=[Y.tag], w=[tmp.tag])

                def fst(e, Y=Y, tmp=tmp, c=c):
                    last = None
                    for si, (t0, n) in enumerate(subs):
                        e.matmul(PS[4 + si][:, 0:n], lhsT=onesf[:, :], rhs=Y.ap[:, t0:t0 + n], start=(c == 0), stop=(c == 7))
                        last = e.matmul(PS[6 + si][:, 0:n], lhsT=onesf[:, :], rhs=tmp.ap[:, t0:t0 + n], start=(c == 0), stop=(c == 7))
                    return last
                op("pe", fst, r=[Y.tag, tmp.tag, "onesf"], w=[("ps", 4), ("ps", 5), ("ps", 6), ("ps", 7)])
            gate_bufs = {}
            for m in range(4):
                gate_bufs[m] = (G_YT[m], G_YT_TAGS[m])
            for co in range(4, 7):
                banks_z = inproj_chunk(l, COL_ZA + co * 128, nb, subs)
                Ug = A_U[co - 4]
                gate_from_psum(banks_z, subs, Ug.ap, Ug.tag)
                gate_bufs[co] = (Ug.ap, [Ug.tag])
            mu, rs = A_MU, A_RS
            for si, (t0, n) in enumerate(subs):
                op("dve", lambda e, si=si, t0=t0, n=n: e.tensor_scalar(out=mu.ap[:, t0:t0 + n], in0=PS[4 + si][:, 0:n], scalar1=1.0 / 1024, scalar2=None, op0=ALU.mult),
                   r=[("ps", 4 + si)], w=[mu.tag])
            tmp = next_tmp()
            op("dve", lambda e, tmp=tmp: e.tensor_tensor(out=tmp.ap[:, t_lo:T], in0=mu.ap[:, t_lo:T], in1=mu.ap[:, t_lo:T], op=ALU.mult), r=[mu.tag], w=[tmp.tag])
            for si, (t0, n) in enumerate(subs):
                op("dve", lambda e, si=si, t0=t0, n=n, tmp=tmp: e.scalar_tensor_tensor(
                    out=rs.ap[:, t0:t0 + n], in0=PS[6 + si][:, 0:n], scalar=1.0 / 1024, in1=tmp.ap[:, t0:t0 + n], op0=ALU.mult, op1=ALU.subtract),
                   r=[("ps", 6 + si), tmp.tag], w=[rs.tag])
            op("act", lambda e: e.activation(out=rs.ap[:, t_lo:T], in_=rs.ap[:, t_lo:T], func=AF.Sqrt, bias=epsL[:, 0:1]), r=[rs.tag, "epsL"], w=[rs.tag])
            op("dve", lambda e: e.reciprocal(out=rs.ap[:, t_lo:T], in_=rs.ap[:, t_lo:T]), r=[rs.tag], w=[rs.tag])
            pending = None
            for c in range(8):
                Y = A_YC[c]
                S = A_S[c]
                op("dve", lambda e, Y=Y: e.tensor_tensor(out=Y.ap[:, t_lo:T], in0=Y.ap[:, t_lo:T], in1=mu.ap[:, t_lo:T], op=ALU.subtract),
                   r=[Y.tag, mu.tag], w=[Y.tag])
                op("dve", lambda e, Y=Y: e.tensor_tensor(out=Y.ap[:, t_lo:T], in0=Y.ap[:, t_lo:T], in1=rs.ap[:, t_lo:T], op=ALU.mult),
                   r=[Y.tag, rs.tag], w=[Y.tag])
                op("dve", lambda e, Y=Y, c=c: e.tensor_scalar(out=Y.ap[:, t_lo:T], in0=Y.ap[:, t_lo:T], scalar1=SMP_LNG(l, c), scalar2=SMP_LNB(l, c), op0=ALU.mult, op1=ALU.add),
                   r=[Y.tag, "smp"], w=[Y.tag])
                tmp = next_tmp()
                op("act", lambda e, Y=Y, tmp=tmp: e.activation(out=tmp.ap[:, t_lo:T], in_=Y.ap[:, t_lo:T], func=AF.Tanh, scale=0.5),
                   r=[Y.tag], w=[tmp.tag])
                if pending is not None:
                    pending()

                def comb(Y=Y, tmp=tmp, S=S):
                    op("dve", lambda e: e.scalar_tensor_tensor(
                        out=S.ap[:, t_lo:T], in0=tmp.ap[:, t_lo:T], scalar=1.0, in1=Y.ap[:, t_lo:T], op0=ALU.add, op1=ALU.mult),
                       r=[tmp.tag, Y.tag], w=[Y.tag])
                pending = comb
            pending()
            banks_z = inproj_chunk(l, COL_ZA + 7 * 128, nb, subs)
            gt7 = next_tmp()
            gate_from_psum(banks_z, subs, gt7.ap, gt7.tag)
            gate_bufs[7] = (gt7.ap, [gt7.tag])
            for co in range(8):
                gap, gtags = gate_bufs[co]
                PW = A_PW[co % 2]
                pwv = PW.ap.rearrange("p (k n) -> p k n", n=128)
                op("pool", lambda e, pwv=pwv, co=co: e.dma_start(out=pwv, in_=conv_pw[l, :, co * 128:(co + 1) * 128].rearrange("(k p) n -> p k n", p=128)),
                   w=[PW.tag], dma=("pws", co % 2))
                bk = (4, 5) if co % 2 == 0 else (6, 7)

                def fpw(e, pwv=pwv, bk=bk):
                    last = None
                    for ci in range(8):
                        for si, (t0, n) in enumerate(subs):
                            last = e.matmul(PS[bk[si]][:, 0:n], lhsT=pwv[:, ci, :], rhs=A_S[ci].ap[:, t0:t0 + n], start=(ci == 0), stop=(ci == 7))
                    return last
                op("pe", fpw, r=[PW.tag] + [A_YC[ci].tag for ci in range(8)], w=[("ps", bk[0]), ("ps", bk[1])])
                for si, (t0, n) in enumerate(subs):
                    op("dve", lambda e, si=si, t0=t0, n=n, bk=bk, gap=gap, co=co: e.scalar_tensor_tensor(
                        out=yT[:, co, t0:t0 + n], in0=PS[bk[si]][:, 0:n], scalar=0.25, in1=gap[:, t0:t0 + n], op0=ALU.mult, op1=ALU.mult),
                       r=[("ps", bk[si])] + gtags, w=[("yT", co, j) for j in range(tb)])

        def branch_b_kv(l, nb, prevT):
            T = nb * 128
            subs = split2(0, T)
            KD = [("kdup", h) for h in range(4)] + [("kdupx", h) for h in range(4)]
            op("dve", lambda e: e.tensor_copy(kdup[:, :, 0:128], kdup[:, :, prevT:prevT + 128]),
               r=["kdup_all"] + KD, w=["kdup_halo"])
            op("dve", lambda e: e.tensor_copy(Vt[:, 0, :], Vt[:, prevT // 128, :]),
               r=["Vt_all"] + [("Vt", j) for j in range(1, tb + 1)], w=[("Vt", 0)])
            for c in range(2):
                banks_k = inproj_chunk(l, COL_K + c * 128, nb, subs)
                for si, (t0, n) in enumerate(subs):
                    op("act", lambda e, si=si, t0=t0, n=n, c=c, banks_k=banks_k: e.activation(
                        out=kdup[0:64, 2 * c, 128 + t0:128 + t0 + n], in_=PS[banks_k[si]][0:64, 0:n], func=AF.Copy),
                       r=[("ps", banks_k[si]), "kdup_halo", "kdup_all"], w=[("kdup", 2 * c)])
                    op("act", lambda e, si=si, t0=t0, n=n, c=c, banks_k=banks_k: e.activation(
                        out=kdup[64:128, 2 * c + 1, 128 + t0:128 + t0 + n], in_=PS[banks_k[si]][64:128, 0:n], func=AF.Copy),
                       r=[("ps", banks_k[si]), "kdup_halo", "kdup_all"], w=[("kdup", 2 * c + 1)])
                op("sp", lambda e, c=c: e.dma_start(out=kdup[64:128, 2 * c, 128:128 + T], in_=kdup[0:64, 2 * c, 128:128 + T]),
                   r=[("kdup", 2 * c), "kdup_halo", "kdup_all"], w=[("kdupx", 2 * c)], dma=("kdupx", 2 * c))
                op("sp", lambda e, c=c: e.dma_start(out=kdup[0:64, 2 * c + 1, 128:128 + T], in_=kdup[64:128, 2 * c + 1, 128:128 + T]),
                   r=[("kdup", 2 * c + 1), "kdup_halo", "kdup_all"], w=[("kdupx", 2 * c + 1)], dma=("kdupx", 2 * c + 1))
            for c in range(2):
                s = next_w()
                wv = Wt[:, s, :, :]
                op("pool", lambda e, wv=wv, c=c: e.dma_start(out=wv, in_=w_in[l, :, COL_V + c * 128:COL_V + (c + 1) * 128].rearrange("(k p) n -> p k n", p=128)),
                   w=[("w", s)], dma=("w", s))
                for j in range(nb):
                    bank = 4 + (state["aux"] % 4)
                    state["aux"] += 1

                    def fv(e, wv=wv, j=j, bank=bank):
                        last = None
                        for kc in range(KC):
                            last = e.matmul(PS[bank][:, 0:128], lhsT=hT[:, kc, j * 128:(j + 1) * 128], rhs=wv[:, kc, :], start=(kc == 0), stop=(kc == KC - 1))
                        return last
                    op("pe", fv, r=[("w", s)] + [("hT", j, q) for q in range(4)], w=[("ps", bank)])
                    op("dve", lambda e, j=j, c=c, bank=bank: e.tensor_copy(Vt[:, 1 + j, c * 128:(c + 1) * 128], PS[bank][:, 0:128]),
                       r=[("ps", bank), "Vt_all", ("Vt", 0)], w=[("Vt", 1 + j)])

        def branch_b_attn(l, b0, nb, i_lo):
            T = nb * 128
            subs = split2(i_lo * 128, T)
            dn, rg = B_DN, B_RG

            def views(g):
                Q, G = B_Q[g % 2], B_G[g % 2]
                return Q, G, Q.ap.rearrange("p (j t) -> p j t", t=TM), G.ap.rearrange("p (j t) -> p j t", t=TM)

            def chunk_emitters(g):
                Q, G, qv, gv = views(g)
                res = []
                for jj in range(4):
                    def fq(jj=jj):
                        banks_q = inproj_chunk(l, COL_Q + (4 * g + jj) * 128, nb, subs)
                        for si, (t0, n) in enumerate(subs):
                            op("act", lambda e, si=si, t0=t0, n=n: e.activation(
                                out=qv[:, jj, t0:t0 + n], in_=PS[banks_q[si]][:, 0:n], func=AF.Copy, scale=0.125),
                               r=[("ps", banks_q[si])], w=[Q.tag])
                    res.append(fq)
                for jj in range(4):
                    def fz(jj=jj):
                        banks_z = inproj_chunk(l, COL_ZB + (4 * g + jj) * 128, nb, subs)
                        gate_from_psum(banks_z, subs, gv[:, jj, :], G.tag)
                    res.append(fz)
                return res

            def att_p1(g, i):
                Q, G, qv, gv = views(g)
                blk = b0 + i
                mprev = 2 if blk == 2 else 0
                for kb, msk in ((0, mprev), (1, 1)):
                    for half in range(2):
                        r0 = 64 * half
                        bank = 4 + (state["aux"] % 2)
                        state["aux"] += 1
                        P = B_PT[kb * 2 + half]

                        def fs(e, bank=bank, msk=msk, kb=kb, r0=r0):
                            e.matmul(PS[bank][:, :], lhsT=identb[:, :], rhs=mask4[:, msk, :], start=True, stop=False)
                            return e.matmul(q4(PS[bank][:, :]),
                                            lhsT=kdup[r0:r0 + 64, g, (i + kb) * 128:(i + kb + 1) * 128],
                                            rhs=qv[r0:r0 + 64, :, i * 128:(i + 1) * 128], start=False, stop=True)
                        op("pe", fs, r=["identb", "mask4", ("kdup", g), ("kdupx", g), "kdup_halo", "kdup_all", Q.tag],
                           w=[("ps", bank)])
                        op("act", lambda e, bank=bank, P=P: e.activation(out=P.ap, in_=PS[bank][:, :], func=AF.Exp),
                           r=[("ps", bank)], w=[P.tag])

            def att_p2(g, i):
                Q, G, qv, gv = views(g)

                def fo(e):
                    last = None
                    for kb in range(2):
                        for half in range(2):
                            r0 = 64 * half
                            e.matmul(PS[6][r0:r0 + 64, :], lhsT=Vt[:, i + kb, g * 64:(g + 1) * 64], rhs=B_PT[kb * 2 + half].ap,
                                     start=(kb == 0), stop=(kb == 1))
                            last = e.matmul(PS[7][r0:r0 + 64, :], lhsT=onesEO[:, half, :], rhs=B_PT[kb * 2 + half].ap,
                                            start=(kb == 0), stop=(kb == 1))
                    return last
                op("pe", fo, r=[P_.tag for P_ in B_PT] + [("Vt", i), ("Vt", i + 1), "Vt_all", "onesEO"], w=[("ps", 6), ("ps", 7)])
                sk = sinkexp[:, l * 16 + 4 * g:l * 16 + 4 * g + 4]
                skb = bass.AP(sk.tensor, sk.offset, [sk.ap[0], [1, 4], [0, 128]])
                op("dve", lambda e: e.tensor_tensor(out=q4(dn.ap), in0=q4(PS[7][:, :]), in1=skb, op=ALU.add),
                   r=[("ps", 7), "sinkexp"], w=[dn.tag])
                op("dve", lambda e: e.reciprocal(out=dn.ap, in_=dn.ap), r=[dn.tag], w=[dn.tag])
                op("dve", lambda e: e.tensor_tensor(out=q4(rg.ap), in0=gv[:, :, i * 128:(i + 1) * 128], in1=q4(dn.ap), op=ALU.mult),
                   r=[dn.tag, G.tag], w=[rg.tag])
                op("dve", lambda e: e.scalar_tensor_tensor(
                    out=yT[:, 8 + 4 * g:8 + 4 * g + 4, i * 128:(i + 1) * 128],
                    in0=q4(PS[6][:, :]), scalar=0.5, in1=q4(rg.ap), op0=ALU.mult, op1=ALU.mult),
                   r=[("ps", 6), rg.tag], w=[("yT", 8 + 4 * g + jj, i) for jj in range(4)])

            blocks = list(range(i_lo, nb))
            for g in range(5):
                cl = chunk_emitters(g) if g < 4 else [(lambda c=c: c_early(l, nb, c)) for c in range(4)]
                al = []
                if g >= 1:
                    ga = g - 1
                    for k in range(len(blocks) + 1):
                        def stage(k=k, ga=ga):
                            if k >= 1:
                                att_p2(ga, blocks[k - 1])
                            if k < len(blocks):
                                att_p1(ga, blocks[k])
                        al.append(stage)
                for k in range(max(len(cl), len(al))):
                    if k < len(cl):
                        cl[k]()
                    if k < len(al):
                        al[k]()

        def c_early(l, nb, c):
            T = nb * 128
            subs_all = split2(0, T)
            UC = C_UC[c]
            uc = UC.ap
            op("dve", lambda e: e.tensor_copy(uc[:, 0:16], uctail[:, c, :]), r=[("uctail", c), "uctail_all"], w=[UC.tag])
            banks_u = inproj_chunk(l, COL_UC + c * 128, nb, subs_all)
            for si, (t0, n) in enumerate(subs_all):
                op("act", lambda e, si=si, t0=t0, n=n: e.activation(
                    out=uc[:, 16 + t0:16 + t0 + n], in_=PS[banks_u[si]][:, 0:n], func=AF.Copy),
                   r=[("ps", banks_u[si])], w=[UC.tag])
            op("dve", lambda e: e.tensor_copy(uctail[:, c, :], uc[:, T:T + 16]), r=[UC.tag], w=[("uctail", c)])

        def branch_c(l, b0, nb, t_lo):
            T = nb * 128
            subs_all = split2(0, T)
            subs = split2(t_lo, T)
            pwc = C_PW.ap.rearrange("p (g k n) -> p g k n", k=2, n=256)
            op("pool", lambda e: e.dma_start(out=pwc, in_=pool_w[l, :, :].rearrange("(g k p) n -> p g k n", k=2, p=128)),
               w=[C_PW.tag], dma="pwc")
            gates = {}
            for c in range(8):
                gp = c // 2
                win = POOL_WINS[gp]
                UC = C_UC[c]
                uc = UC.ap
                if c >= 4:
                    c_early(l, nb, c)
                src, stag = uc, UC.tag
                nsteps = {2: 1, 4: 2, 8: 3, 16: 4}[win]
                si_ = 0
                for k in range(1, nsteps + 1):
                    sh = 1 << (k - 1)
                    lo = (1 << k) - 1
                    dst, dtag = C_S[si_].ap, C_S[si_].tag
                    op("dve", lambda e, src=src, dst=dst, sh=sh, lo=lo: e.tensor_tensor(
                        out=dst[:, lo:16 + T], in0=src[:, lo:16 + T], in1=src[:, lo - sh:16 + T - sh], op=ALU.add),
                       r=[stag], w=[dtag])
                    src, stag = dst, dtag
                    si_ ^= 1
                MX = C_MX[c]
                op("dve", lambda e, src=src, uc=uc, MX=MX, win=win: e.scalar_tensor_tensor(
                    out=MX.ap[:, 0:T], in0=src[:, 16:16 + T], scalar=1.0 / win, in1=uc[:, 16:16 + T], op0=ALU.mult, op1=ALU.subtract),
                   r=[stag, UC.tag], w=[MX.tag])
                if b0 <= 2 < b0 + nb:
                    off = (2 - b0) * 128
                    iv = cst[:, 512 + gp * 16:512 + gp * 16 + 16]
                    tmpd, tdtag = C_S[si_].ap, C_S[si_].tag
                    op("dve", lambda e, src=src, off=off, iv=iv, tmpd=tmpd: e.tensor_tensor(
                        out=tmpd[:, 0:16], in0=src[:, 16 + off:32 + off], in1=iv, op=ALU.mult),
                       r=[stag, "cst"], w=[tdtag])
                    op("dve", lambda e, tmpd=tmpd, uc=uc, MX=MX, off=off: e.tensor_tensor(
                        out=MX.ap[:, off:off + 16], in0=tmpd[:, 0:16], in1=uc[:, 16 + off:32 + off], op=ALU.subtract),
                       r=[tdtag, UC.tag], w=[MX.tag])
                banks_z = inproj_chunk(l, COL_ZC + c * 128, nb, subs)
                gt = next_tmp()
                gate_from_psum(banks_z, subs, gt.ap, gt.tag)
                gates[c] = gt
                if c % 2 == 1:
                    for dch in range(2):
                        cc = 2 * gp + dch
                        bk = (4, 5) if dch == 0 else (6, 7)

                        def fpl(e, gp=gp, dch=dch, bk=bk):
                            last = None
                            for k in range(2):
                                for si, (t0, n) in enumerate(subs):
                                    last = e.matmul(PS[bk[si]][:, 0:n], lhsT=pwc[:, gp, k, dch * 128:(dch + 1) * 128],
                                                    rhs=C_MX[2 * gp + k].ap[:, t0:t0 + n], start=(k == 0), stop=(k == 1))
                            return last
                        op("pe", fpl, r=[C_PW.tag, C_MX[2 * gp].tag, C_MX[2 * gp + 1].tag], w=[("ps", bk[0]), ("ps", bk[1])])
                        gt_ = gates[cc]
                        for si, (t0, n) in enumerate(subs):
                            op("dve", lambda e, si=si, t0=t0, n=n, bk=bk, gt_=gt_, cc=cc: e.scalar_tensor_tensor(
                                out=yT[:, 24 + cc, t0:t0 + n], in0=PS[bk[si]][:, 0:n], scalar=SMP_PSC(l, cc), in1=gt_.ap[:, t0:t0 + n],
                                op0=ALU.mult, op1=ALU.mult),
                               r=[("ps", bk[si]), gt_.tag, "smp"], w=[("yT", 24 + cc, j) for j in range(nb)])

        def out_proj(l, xsrc, xdst, b0, nb, j_lo):
            nj = nb - j_lo
            WO = [AV(0, 8192, BF16), AV(4096, 8192, BF16)]
            XR = [AV(8192, tb * 256), AV(8192 + tb * 256, tb * 256)]
            XO = [AV(8192 + 2 * tb * 256, tb * 256), AV(8192 + 3 * tb * 256, tb * 256)]
            assert 8192 + 4 * tb * 256 <= ARENA_F32
            r0, r1 = (b0 + j_lo) * 128, (b0 + nb) * 128

            def prefetch(cb):
                W = WO[cb % 2]
                wo = W.ap.rearrange("p (k n) -> p k n", n=256)
                op("pool", lambda e, wo=wo, cb=cb: e.dma_start(out=wo, in_=w_out[l, :, cb * 256:(cb + 1) * 256].rearrange("(k p) n -> p k n", p=128)),
                   w=[W.tag], dma=("wo", cb % 2))
                X = XR[cb % 2]
                xr = X.ap[:, 0:nj * 256].rearrange("p (j n) -> p j n", n=256)
                op("sp", lambda e, xr=xr, cb=cb: e.dma_start(out=xr, in_=xsrc[r0:r1, cb * 256:(cb + 1) * 256].rearrange("(j p) n -> p j n", p=128)),
                   r=[("x", id(xsrc), b0 + j, cb) for j in range(j_lo, nb)], w=[X.tag], dma=("xr", cb % 2))
            prefetch(0)
            gi = 0
            for cb in range(16):
                if cb + 1 < 16:
                    prefetch(cb + 1)
                W, X, O = WO[cb % 2], XR[cb % 2], XO[cb % 2]
                wo = W.ap.rearrange("p (k n) -> p k n", n=256)
                xr = X.ap[:, 0:nj * 256].rearrange("p (j n) -> p j n", n=256)
                xo = O.ap[:, 0:nj * 256].rearrange("p (j n) -> p j n", n=256)
                for j in range(j_lo, nb):
                    bank = gi % 4
                    gi += 1

                    def fop(e, wo=wo, j=j, bank=bank):
                        last = None
                        for kc in range(KC):
                            last = e.matmul(PS[bank][:, 0:256], lhsT=yT[:, kc, j * 128:(j + 1) * 128], rhs=wo[:, kc, :], start=(kc == 0), stop=(kc == KC - 1))
                        return last
                    op("pe", fop, r=[W.tag] + [("yT", c, j) for c in range(KC)], w=[("ps", bank)])
                    op("dve", lambda e, xr=xr, xo=xo, bank=bank, j=j: e.tensor_tensor(out=xo[:, j - j_lo, :], in0=PS[bank][:, 0:256], in1=xr[:, j - j_lo, :], op=ALU.add),
                       r=[("ps", bank), X.tag], w=[O.tag])
                op("act", lambda e, xo=xo, cb=cb: e.dma_start(out=xdst[r0:r1, cb * 256:(cb + 1) * 256].rearrange("(j p) n -> p j n", p=128), in_=xo),
                   r=[O.tag], w=[("x", id(xdst), b0 + j, cb) for j in range(j_lo, nb)], dma=("xo", cb % 2))

        def final_phase(xsrc):
            XT = [AV(0, 4096), AV(4096, 4096), AV(8192, 4096)]
            JK = AV(12288, 2048, BF16)
            gbf = Wt.bitcast(F32)[:, 0:2, :, :].rearrange("p a k n -> p (a k n)")
            GBT = [("w", 0), ("w", 1)]
            op("sp", lambda e: e.dma_start(out=gbf, in_=norm_g[2:3, :].partition_broadcast(128)), w=GBT, dma="gbf")
            for j in range(n_real_blk):
                blk = 2 + j
                xs = j % 3
                xt = XT[xs]
                op("sp", lambda e, xt=xt, blk=blk: e.dma_start(out=xt.ap, in_=xsrc[blk * 128:(blk + 1) * 128, :]),
                   r=[("x", id(xsrc), blk, cb) for cb in range(16)], w=[xt.tag], dma=("xtf", xs))
                ssA = stat[:, 3 * tb + 0:3 * tb + 1]
                ssB = stat[:, 3 * tb + 1:3 * tb + 2]
                ss = stat[:, 3 * tb + 2 + xs:3 * tb + 3 + xs]
                op("act", lambda e, xt=xt, ssA=ssA: e.activation(out=JK.ap, in_=xt.ap[:, 0:2048], func=AF.Square, accum_out=ssA),
                   r=[xt.tag], w=[JK.tag, "ssfA"])
                op("act", lambda e, xt=xt, ssB=ssB: e.activation(out=JK.ap, in_=xt.ap[:, 2048:4096], func=AF.Square, accum_out=ssB),
                   r=[xt.tag], w=[JK.tag, "ssfB"])
                op("dve", lambda e, ss=ss, ssA=ssA, ssB=ssB: e.tensor_tensor(out=ss, in0=ssA, in1=ssB, op=ALU.add),
                   r=["ssfA", "ssfB"], w=[("ssf", xs)])
                op("act", lambda e, ss=ss: e.activation(out=ss, in_=ss, func=AF.Sqrt, scale=1.0 / D, bias=epsN[:, 0:1]),
                   r=[("ssf", xs), "epsN"], w=[("ssf", xs)])
                op("dve", lambda e, ss=ss: e.reciprocal(out=ss, in_=ss), r=[("ssf", xs)], w=[("ssf", xs)])
                op("dve", lambda e, xt=xt, ss=ss: e.scalar_tensor_tensor(out=xt.ap, in0=xt.ap, scalar=ss, in1=gbf, op0=ALU.mult, op1=ALU.mult),
                   r=[xt.tag, ("ssf", xs)] + GBT, w=[xt.tag])
                op("pool", lambda e, xt=xt, j=j: e.dma_start(out=out[j * 128:(j + 1) * 128, :], in_=xt.ap),
                   r=[xt.tag], w=[("out", j)], dma=("xto", xs))

        def tiles(start):
            res = []
            b = start
            while b < NBE:
                n = min(tb, NBE - b)
                res.append((b, n))
                b += n
            return res

        srcs = [x_ext, x1, x2]
        for l in range(n_layers):
            if l > 0:
                KD = [("kdup", h) for h in range(4)] + [("kdupx", h) for h in range(4)]
                op("dve", lambda e: e.memset(kdup[:, :, :], 0.0), w=KD + ["kdup_halo", "kdup_all"])
                op("dve", lambda e: e.memset(Vt[:, :, :], 0.0), w=[("Vt", j) for j in range(tb + 1)] + ["Vt_all"])
                op("dve", lambda e: e.memset(utail[:, :, :], 0.0), w=[("utail", c) for c in range(8)] + ["utail_all"])
                op("dve", lambda e: e.memset(uctail[:, :, :], 0.0), w=[("uctail", c) for c in range(8)] + ["uctail_all"])
            prevT = tb * 128
            for ti, (b0, nb) in enumerate(tiles(l)):
                j_lo = 1 if ti == 0 else 0
                t_lo = 128 * j_lo
                norm_phase(srcs[l], l, b0, nb)
                branch_a1(l, nb, t_lo)
                branch_b_kv(l, nb, prevT)
                branch_a2(l, nb, t_lo)
                branch_b_attn(l, b0, nb, j_lo)
                branch_c(l, b0, nb, t_lo)
                out_proj(l, srcs[l], srcs[l + 1], b0, nb, j_lo)
                prevT = nb * 128
        final_phase(srcs[n_layers])

        _emit(nc, ops, es)
    return nc


def _host_consts(core_at_seq_start):
    c = np.zeros((128, 576), np.float32)
    c[:, 0:128] = np.eye(128, dtype=np.float32)
    j = np.arange(128)[:, None]
    i = np.arange(128)[None, :]
    maskP = np.where(j >= i, 0.0, NEG).astype(np.float32)
    maskC = np.where(j <= i, 0.0, NEG).astype(np.float32)
    c[:, 128:256] = maskP
    c[:, 256:384] = maskC
    c[:, 384:512] = NEG if core_at_seq_start else maskP
    for g, win in enumerate(POOL_WINS):
        pos = np.arange(1, 17, dtype=np.float32)
        div = np.minimum(pos, float(win)) if core_at_seq_start else np.full(16, float(win), np.float32)
        c[:, 512 + g * 16:512 + (g + 1) * 16] = (1.0 / div)[None, :]
    return c


def _host_small(conv_dw, conv_ln_g, conv_ln_b, pool_scale, attn_sinks):
    s = np.zeros((128, 576), np.float32)
    dw = np.transpose(conv_dw.reshape(2, 31, 8, 128), (3, 0, 2, 1))
    s[:, 0:496] = dw.reshape(128, 496)
    s[:, 496:512] = np.transpose(conv_ln_g.reshape(2, 8, 128), (2, 0, 1)).reshape(128, 16)
    s[:, 512:528] = np.transpose(conv_ln_b.reshape(2, 8, 128), (2, 0, 1)).reshape(128, 16)
    s[:, 528:544] = np.transpose(pool_scale.reshape(2, 8, 128), (2, 0, 1)).reshape(128, 16)
    sk = attn_sinks.reshape(2, 16, 2)
    s[0:64, 544:576] = sk[:, :, 0].reshape(1, 32)
    s[64:128, 544:576] = sk[:, :, 1].reshape(1, 32)
    return s


_NC_CACHE = {}


def _run(inputs, n_cores, tok_per_core, seq_len, tb=6, n_layers=2):
    x = np.asarray(inputs["x"], np.float32)
    B, S, _ = x.shape
    nrb = tok_per_core // 128
    key = (nrb, tb, n_layers)
    if key not in _NC_CACHE:
        _NC_CACHE[key] = build_program(nrb, tb, n_layers)
    nc = _NC_CACHE[key]
    xf = x.reshape(B * S, D)
    f32 = lambda a: np.ascontiguousarray(np.asarray(a, np.float32))
    w_in = f32(inputs["w_in"])
    conv_pw = f32(inputs["conv_pw"])
    pool_w = f32(inputs["pool_w"]).reshape(2, 1024, 256)
    w_out = f32(inputs["w_out"])
    norm_g = np.concatenate([f32(inputs["norm_g"]), f32(inputs["final_norm_g"])[None, :]], axis=0)
    small = _host_small(f32(inputs["conv_dw"]), f32(inputs["conv_ln_g"]), f32(inputs["conv_ln_b"]),
                        f32(inputs["pool_scale"]), f32(inputs["attn_sinks"]))
    in_maps = []
    for c in range(n_cores):
        t0 = c * tok_per_core
        at_start = (t0 % seq_len) == 0
        xe = np.zeros((256 + tok_per_core, D), np.float32)
        xe[256:] = xf[t0:t0 + tok_per_core]
        if not at_start:
            xe[0:256] = xf[t0 - 256:t0]
        in_maps.append({"x_ext": xe, "w_in": w_in, "conv_pw": conv_pw, "pool_w": pool_w, "w_out": w_out,
                        "norm_g": norm_g, "smallp": small, "cpk": _host_consts(at_start)})
    res = run_bass_kernel_spmd(nc, in_maps, core_ids=list(range(n_cores)))
    outs = [np.asarray(r["out"], np.float32) for r in res.results]
    return np.concatenate(outs, axis=0)


def kernel(x, norm_g, w_in, conv_dw, conv_ln_g, conv_ln_b, conv_pw, attn_sinks, pool_w, pool_scale, w_out, final_norm_g):
    inputs = dict(x=x, norm_g=norm_g, w_in=w_in, conv_dw=conv_dw, conv_ln_g=conv_ln_g, conv_ln_b=conv_ln_b,
                  conv_pw=conv_pw, attn_sinks=attn_sinks, pool_w=pool_w, pool_scale=pool_scale, w_out=w_out,
                  final_norm_g=final_norm_g)
    xs = np.asarray(x)
    B, S, _ = xs.shape
    o = _run(inputs, 8, (B * S) // 8, S)
    return o.reshape(B, S, D).astype(np.float32)
```
